# Optimizing a Trainium2 kernel written in Bass

```python
import jax, jax.numpy as jnp
from jax import lax
import numpy as np

D_MODEL = 1024
BATCH = 4
SEQ = 4096
DEPTH = 1

CTX_LEN = 256
GRID_W = 64
D_INNER = 2 * D_MODEL
W_POOL = D_INNER // 2
W_SSD = D_INNER - W_POOL
POOL_WINDOWS = (2, 4, 8, 16)
N_POOL_GROUPS = len(POOL_WINDOWS)
POOL_GROUP_W = W_POOL // N_POOL_GROUPS
SSD_HEADDIM = 64
SSD_HEADS = W_SSD // SSD_HEADDIM
SSD_GROUPS = 4
SSD_HEADS_PER_GROUP = SSD_HEADS // SSD_GROUPS
D_STATE = 128
D_CONV = 4
CONV_LEFT = D_CONV // 2
CHUNK = 128
N_DIR = 2
GN = SSD_GROUPS * D_STATE
CONV_DIM = W_SSD + 2 * GN
OFF_POOL_Z = W_POOL
OFF_SSD_Z = 2 * W_POOL
OFF_XBC = 2 * W_POOL + W_SSD
OFF_DT = OFF_XBC + CONV_DIM
PROJ_DIM = OFF_DT + N_DIR * SSD_HEADS
EPS = 1e-6

kernel_name = "hybrid_pool_ssd_prefix_dit_block"


def rmsnorm(x, w):
    xf = x.astype(jnp.float32)
    y = xf * lax.rsqrt(jnp.mean(xf * xf, axis=-1, keepdims=True) + EPS)
    return (y * w.astype(jnp.float32)).astype(x.dtype)


def box_mean(x, window, axis):
    n = x.shape[axis]
    s = jnp.cumsum(x.astype(jnp.float32), axis=axis)
    pad = [(0, 0)] * x.ndim
    pad[axis] = (1, 0)
    s = jnp.pad(s, pad)
    t = jnp.arange(n)
    lo = jnp.clip(t - window // 2, 0, n)
    hi = jnp.clip(t + window - window // 2, 0, n)
    total = jnp.take(s, hi, axis=axis) - jnp.take(s, lo, axis=axis)
    shape = [1] * x.ndim
    shape[axis] = n
    count = (hi - lo).astype(jnp.float32).reshape(shape)
    return (total / count).astype(x.dtype)


def pool_mixer(u, w_lin, scale, grid):
    b, n, _ = u.shape
    groups = u.reshape(b, n, N_POOL_GROUPS, POOL_GROUP_W)
    outs = []
    for g, w in enumerate(POOL_WINDOWS):
        ug = groups[:, :, g]
        if grid:
            rows = n // GRID_W
            img = ug.reshape(b, rows, GRID_W, POOL_GROUP_W)
            m = box_mean(box_mean(img, w, 1), w, 2).reshape(b, n, POOL_GROUP_W)
        else:
            m = box_mean(ug, w, 1)
        outs.append(m - ug)
    d = jnp.stack(outs, axis=2)
    y = jnp.einsum('bngc,gcd->bngd', d, w_lin).reshape(b, n, W_POOL)
    return y * scale


def centred_dwconv(u, w, bias):
    n = u.shape[1]
    up = jnp.pad(u, ((0, 0), (CONV_LEFT, D_CONV - 1 - CONV_LEFT), (0, 0)))
    y = sum(up[:, k:k + n] * w[k] for k in range(D_CONV))
    return jax.nn.silu(y + bias)


def ssd_scan(x, dt, a, b_in, c_in, h0):
    bsz, n = x.shape[:2]
    nc = n // CHUNK
    G, R = SSD_GROUPS, SSD_HEADS_PER_GROUP
    xd = (x * dt[..., None]).reshape(bsz, nc, CHUNK, G, R, SSD_HEADDIM)
    adt = (dt * a).reshape(bsz, nc, CHUNK, G, R)
    bc = b_in.reshape(bsz, nc, CHUNK, G, D_STATE)
    cc = c_in.reshape(bsz, nc, CHUNK, G, D_STATE)
    acum = jnp.cumsum(adt, axis=2)
    seg = acum[:, :, :, None] - acum[:, :, None, :]
    lower = jnp.tril(jnp.ones((CHUNK, CHUNK), dtype=bool))[:, :, None, None]
    decay = jnp.exp(jnp.where(lower, seg, -jnp.inf))
    cb = jnp.einsum('bclgn,bcsgn->bclsg', cc, bc)
    y_diag = jnp.einsum('bclsg,bclsgr,bcsgrp->bclgrp', cb, decay, xd)
    decay_to_end = jnp.exp(acum[:, :, -1:] - acum)
    chunk_states = jnp.einsum('bclgn,bclgr,bclgrp->bcgrpn', bc, decay_to_end, xd)
    chunk_decay = jnp.exp(acum[:, :, -1])

    def step(h, inp):
        s_c, d_c = inp
        return h * d_c[..., None, None] + s_c, h

    h_final, h_starts = lax.scan(
        step, h0, (jnp.moveaxis(chunk_states, 1, 0), jnp.moveaxis(chunk_decay, 1, 0)))
    h_starts = jnp.moveaxis(h_starts, 0, 1)
    y_off = jnp.einsum('bclgn,bcgrpn,bclgr->bclgrp', cc, h_starts, jnp.exp(acum))
    y = (y_diag + y_off).reshape(bsz, n, SSD_HEADS, SSD_HEADDIM)
    return y, h_final


def ssd_bidir(xbc_raw, dt_raw, conv_w, conv_b, a_log, dt_bias, d_skip, h0):
    bsz, n, _ = xbc_raw.shape
    xbc = centred_dwconv(xbc_raw, conv_w, conv_b)
    xs = xbc[..., :W_SSD].reshape(bsz, n, SSD_HEADS, SSD_HEADDIM)
    bs = xbc[..., W_SSD:W_SSD + GN].reshape(bsz, n, SSD_GROUPS, D_STATE)
    cs = xbc[..., W_SSD + GN:].reshape(bsz, n, SSD_GROUPS, D_STATE)
    dt = jax.nn.softplus(dt_raw.reshape(bsz, n, N_DIR, SSD_HEADS).astype(jnp.float32)
                         + dt_bias.astype(jnp.float32))
    a = -jnp.exp(a_log.astype(jnp.float32))
    flip = lambda t: jnp.flip(t, axis=1)
    y_f, h_f = ssd_scan(xs, dt[:, :, 0], a[0], bs, cs, h0[0])
    y_b, h_b = ssd_scan(flip(xs), flip(dt[:, :, 1]), a[1], flip(bs), flip(cs), h0[1])
    y = y_f + flip(y_b) + d_skip[:, None] * xs
    return y.reshape(bsz, n, W_SSD).astype(xbc_raw.dtype), jnp.stack([h_f, h_b])


def modulated_projection(h, norm_w, shift, scale, w_in):
    hm = rmsnorm(h, norm_w) * (1.0 + scale) + shift
    return hm @ w_in


def mixer_output(p, y_ssd, grid, pool_w, pool_scale, ssd_norm_w, w_out):
    bsz, n, _ = p.shape
    u_pool = p[..., :W_POOL]
    z_pool = p[..., OFF_POOL_Z:OFF_SSD_Z]
    z_ssd = p[..., OFF_SSD_Z:OFF_XBC]
    y_pool = pool_mixer(u_pool, pool_w, pool_scale, grid) * jax.nn.silu(z_pool)
    gated = (y_ssd * jax.nn.silu(z_ssd)).reshape(bsz, n, SSD_GROUPS, W_SSD // SSD_GROUPS)
    y_s = rmsnorm(gated, ssd_norm_w.reshape(SSD_GROUPS, W_SSD // SSD_GROUPS)).reshape(bsz, n, W_SSD)
    return jnp.concatenate([y_pool, y_s], axis=-1) @ w_out


def setup_inputs(seed: int = 0) -> dict:
    key = jax.random.key(seed)
    ks = jax.random.split(key, 20)
    nrm = jax.random.normal
    x = nrm(ks[0], (BATCH, SEQ, D_MODEL), jnp.float32)
    c = nrm(ks[1], (BATCH, D_MODEL), jnp.float32)
    ctx = nrm(ks[2], (BATCH, CTX_LEN, D_MODEL), jnp.float32)
    c_ctx = nrm(ks[3], (D_MODEL,), jnp.float32)
    norm_w = 1.0 + 0.02 * nrm(ks[4], (DEPTH, D_MODEL), jnp.float32)
    w_ada = 0.5 * D_MODEL ** -0.5 * nrm(ks[5], (DEPTH, D_MODEL, 3 * D_MODEL), jnp.float32)
    b_ada = 0.02 * nrm(ks[6], (DEPTH, 3 * D_MODEL), jnp.float32)
    w_in = D_MODEL ** -0.5 * nrm(ks[7], (DEPTH, D_MODEL, PROJ_DIM), jnp.float32)
    conv_w = D_CONV ** -0.5 * nrm(ks[8], (DEPTH, D_CONV, CONV_DIM), jnp.float32)
    conv_b = 0.02 * nrm(ks[9], (DEPTH, CONV_DIM), jnp.float32)
    a_log = jnp.log(jax.random.uniform(ks[10], (DEPTH, N_DIR, SSD_HEADS), jnp.float32, 1.0, 16.0))
    dt0 = jnp.exp(jax.random.uniform(ks[11], (DEPTH, N_DIR, SSD_HEADS), jnp.float32,
                                     float(np.log(1e-3)), float(np.log(1e-1))))
    dt_bias = dt0 + jnp.log(-jnp.expm1(-dt0))
    d_skip = 1.0 + 0.02 * nrm(ks[12], (DEPTH, SSD_HEADS), jnp.float32)
    ssd_norm_w = 1.0 + 0.02 * nrm(ks[13], (DEPTH, W_SSD), jnp.float32)
    pool_w = POOL_GROUP_W ** -0.5 * nrm(ks[14], (DEPTH, N_POOL_GROUPS, POOL_GROUP_W, POOL_GROUP_W), jnp.float32)
    pool_scale = 1.0 + 0.02 * nrm(ks[15], (DEPTH, W_POOL), jnp.float32)
    w_out = D_INNER ** -0.5 * nrm(ks[16], (DEPTH, D_INNER, D_MODEL), jnp.float32)
    final_norm_w = 1.0 + 0.02 * nrm(ks[17], (D_MODEL,), jnp.float32)
    return {"x": x, "c": c, "ctx": ctx, "c_ctx": c_ctx, "norm_w": norm_w,
            "w_ada": w_ada, "b_ada": b_ada, "w_in": w_in, "conv_w": conv_w,
            "conv_b": conv_b, "a_log": a_log, "dt_bias": dt_bias, "d_skip": d_skip,
            "ssd_norm_w": ssd_norm_w, "pool_w": pool_w, "pool_scale": pool_scale,
            "w_out": w_out, "final_norm_w": final_norm_w}


def reference(x, c, ctx, c_ctx, norm_w, w_ada, b_ada, w_in, conv_w, conv_b, a_log,
              dt_bias, d_skip, ssd_norm_w, pool_w, pool_scale, w_out, final_norm_w):
    bsz = x.shape[0]
    h_lat, h_ctx = x, ctx
    for i in range(DEPTH):
        mod_lat = jax.nn.silu(c) @ w_ada[i] + b_ada[i]
        mod_ctx = jax.nn.silu(c_ctx) @ w_ada[i] + b_ada[i]
        sh_l, sc_l, g_l = jnp.split(mod_lat[:, None, :], 3, axis=-1)
        sh_c, sc_c, g_c = jnp.split(mod_ctx, 3, axis=-1)

        p_ctx = modulated_projection(h_ctx, norm_w[i], sh_c, sc_c, w_in[i])
        h0 = jnp.zeros((N_DIR, bsz, SSD_GROUPS, SSD_HEADS_PER_GROUP, SSD_HEADDIM, D_STATE), jnp.float32)
        y_ssd_ctx, h_ctx_end = ssd_bidir(p_ctx[..., OFF_XBC:OFF_DT], p_ctx[..., OFF_DT:],
                                         conv_w[i], conv_b[i], a_log[i], dt_bias[i], d_skip[i], h0)

        p_lat = modulated_projection(h_lat, norm_w[i], sh_l, sc_l, w_in[i])
        y_ssd_lat, _ = ssd_bidir(p_lat[..., OFF_XBC:OFF_DT], p_lat[..., OFF_DT:],
                                 conv_w[i], conv_b[i], a_log[i], dt_bias[i], d_skip[i], h_ctx_end)
        h_lat = h_lat + g_l * mixer_output(p_lat, y_ssd_lat, True, pool_w[i], pool_scale[i],
                                           ssd_norm_w[i], w_out[i])
        if i < DEPTH - 1:
            h_ctx = h_ctx + g_c * mixer_output(p_ctx, y_ssd_ctx, False, pool_w[i], pool_scale[i],
                                               ssd_norm_w[i], w_out[i])
    return rmsnorm(h_lat, final_norm_w)
```

```python
import numpy as np
import ml_dtypes
from contextlib import ExitStack
import concourse.bass as bass
import concourse.mybir as mybir
from concourse.bass_utils import run_bass_kernel_spmd

F32 = mybir.dt.float32
BF16 = mybir.dt.bfloat16
ALU = mybir.AluOpType
AF = mybir.ActivationFunctionType

DEBUG = {}
ENGS = ("pe", "act", "dve", "pool", "sp")
N_DMA_SEMS = 24
BIG = 1.0e30
EPS = 1e-6

D = 1024
SEQ = 4096
TOWN = 2048
NTILE = 16
XW = 2560
CTX = 256
PROJ = 5152
OFF_PZ, OFF_SZ, OFF_XBC, OFF_DT = 1024, 2048, 3072, 5120


def ap_key(ap):
    dims = list(ap.ap)
    off = int(ap.offset)
    if "DRAM" not in str(ap.space).upper():
        ps = int(dims[0][0])
        if ps > 0:
            off = off % ps
    span = 0
    for st, cnt in dims[1:]:
        span += abs(int(st)) * (int(cnt) - 1)
    return ap.tensor.name, off, off + span


class Op:
    __slots__ = ("eng", "fn", "deps", "signal", "sig_idx", "is_dma", "dma_sem", "dma_val", "idx", "raw", "pos")

    def __init__(self, eng, fn, is_dma):
        self.eng, self.fn, self.is_dma = eng, fn, is_dma
        self.deps = set()
        self.raw = set()
        self.pos = -1
        self.signal = False
        self.sig_idx = 0
        self.dma_sem = -1
        self.dma_val = 0
        self.idx = -1


class Sched:
    def __init__(self, nc):
        self.nc = nc
        self.ops = []
        self.hist = {}
        self.n_dma = 0
        self.marks = []

    def mark(self, name):
        self.marks.append((name, len(self.ops)))

    def _touch(self, op, aps, is_write):
        for ap in aps:
            name, lo, hi = ap_key(ap)
            real_w = is_write
            if "PSUM" in str(ap.space).upper():
                lo, hi, is_write = 0, 1 << 30, True
            lst = self.hist.get(name, [])
            keep = []
            for e in lst:
                elo, ehi, eidx, ew, erw = e
                if eidx != op.idx:
                    if not (ehi < lo or hi < elo) and (is_write or ew):
                        op.deps.add(eidx)
                        if erw and not real_w:
                            op.raw.add(eidx)
                    if is_write and lo <= elo and ehi <= hi:
                        continue
                keep.append(e)
            keep.append((lo, hi, op.idx, is_write, real_w))
            self.hist[name] = keep

    def add(self, eng, fn, reads=(), writes=(), dma=False):
        op = Op(eng, fn, dma)
        op.idx = len(self.ops)
        self.ops.append(op)
        self._touch(op, [a for a in reads if a is not None], False)
        self._touch(op, [a for a in writes if a is not None], True)
        if dma:
            op.dma_sem = self.n_dma % N_DMA_SEMS
            op.dma_val = 16 * (self.n_dma // N_DMA_SEMS + 1)
            self.n_dma += 1
        return op

    def emit(self):
        nc, ops = self.nc, self.ops
        last_on_sem = {}
        for op in ops:
            if op.is_dma:
                p = last_on_sem.get(op.dma_sem)
                if p is not None:
                    op.deps.add(p)
                last_on_sem[op.dma_sem] = op.idx
        per_eng0 = {e: [o for o in ops if o.eng == e] for e in ENGS}
        for e in ENGS:
            for i, o in enumerate(per_eng0[e]):
                o.pos = i

        def same_engine_ok(op, dop):
            if op.is_dma or dop.is_dma or dop.eng != op.eng:
                return False
            if op.eng == "pe":
                return True
            if op.eng in ("act", "dve"):
                return (dop.idx not in op.raw) or (op.pos - dop.pos >= 2)
            return False

        self._same_ok = same_engine_ok
        for op in ops:
            for d in op.deps:
                dop = ops[d]
                if not dop.is_dma:
                    if same_engine_ok(op, dop):
                        continue
                    dop.signal = True
        cnt = {e: 0 for e in ENGS}
        for op in ops:
            if op.signal and not op.is_dma:
                cnt[op.eng] += 1
                op.sig_idx = cnt[op.eng]
        per_eng = {e: [o for o in ops if o.eng == e] for e in ENGS}
        dma_tot = {}
        for op in ops:
            if op.is_dma:
                dma_tot[op.dma_sem] = max(dma_tot.get(op.dma_sem, 0), op.dma_val)
        with ExitStack() as st:
            esem = {e: st.enter_context(nc.semaphore("s_" + e)) for e in ENGS}
            dsem = [st.enter_context(nc.semaphore("d%d" % i)) for i in range(N_DMA_SEMS)]
            block = st.enter_context(nc.Block())
            engobj = {"pe": nc.tensor, "act": nc.scalar, "dve": nc.vector, "pool": nc.gpsimd, "sp": nc.sync}

            def run(e):
                eng = engobj[e]
                waited = {}
                for op in per_eng[e]:
                    need = {}
                    for d in op.deps:
                        dop = ops[d]
                        if dop.is_dma:
                            k, v = ("d", dop.dma_sem), dop.dma_val
                        else:
                            if same_engine_ok(op, dop):
                                continue
                            k, v = ("e", dop.eng), dop.sig_idx
                        if need.get(k, 0) < v:
                            need[k] = v
                    for k, v in need.items():
                        if waited.get(k, 0) >= v:
                            continue
                        waited[k] = v
                        eng.wait_ge(dsem[k[1]] if k[0] == "d" else esem[k[1]], v)
                    ins = op.fn(eng)
                    if op.is_dma:
                        ins.then_inc(dsem[op.dma_sem], 16)
                    elif op.signal:
                        ins.then_inc(esem[e], 1)
                if e == "sp":
                    for s, tot in dma_tot.items():
                        eng.wait_ge(dsem[s], tot)

            @block.sync
            def _(sync):
                run("sp")

            @block.scalar
            def _(scalar):
                run("act")

            @block.vector
            def _(vector):
                run("dve")

            @block.gpsimd
            def _(gpsimd):
                run("pool")

            @block.tensor
            def _(tensor):
                run("pe")


def _pool_tables(half):
    wins = (2, 4, 8, 16)
    nspec = (1, 1, 2, 4)
    drange = ((-1, 1), (-1, 1), (-2, 2), (-4, 4))
    mats, midx, invs, iidx = [], {}, [], {}
    for g, w in enumerate(wins):
        lo_off, hi_off = (-(w // 2), w - w // 2 - 1) if half == 0 else (-(w - w // 2 - 1), w // 2)

        def rng_(r):
            return max(r + lo_off, 0), min(r + hi_off, 63)

        def mat(j, d):
            i = j + d
            M = np.zeros((128, 128), np.float32)
            cnt = np.zeros(128, np.float64)
            for t in range(128):
                r, c = 2 * j + t // 64, t % 64
                r0, r1 = rng_(r)
                c0, c1 = rng_(c)
                cnt[t] = (r1 - r0 + 1) * (c1 - c0 + 1)
                for rp in range(max(r0, 2 * i), min(r1, 2 * i + 1) + 1):
                    M[(rp - 2 * i) * 64 + c0:(rp - 2 * i) * 64 + c1 + 1, t] = 1.0
                if d == 0:
                    M[t, t] -= cnt[t]
            return M, cnt

        band = {}
        for d in range(drange[g][0], drange[g][1] + 1):
            if d == 0:
                continue
            band[d] = len(mats)
            mats.append(mat(8, d)[0])
        dg, iv = {}, {}
        for jc in range(nspec[g] + 1):
            M, cnt = mat(jc, 0)
            dg[jc] = len(mats)
            mats.append(M)
            iv[jc] = len(invs)
            invs.append((1.0 / cnt).astype(np.float32))
        for j in range(NTILE):
            jc = min(j, nspec[g])
            iidx[(g, j)] = iv[jc]
            for d in range(drange[g][0], drange[g][1] + 1):
                if j + d < 0:
                    continue
                midx[(g, j, d)] = dg[jc] if d == 0 else band[d]
    mats = np.stack(mats, 1).astype(ml_dtypes.bfloat16)
    invs = np.broadcast_to(np.stack(invs, 0)[None], (128, len(invs), 128)).astype(np.float32).copy()
    return mats, midx, invs, iidx


def _consts():
    ident = np.eye(128, dtype=np.float32)
    s = np.arange(128)[:, None]
    l = np.arange(128)[None, :]
    maskf = (l >= s).astype(np.float32)
    maskb = (l <= s).astype(np.float32)
    sel = np.zeros((96, 32, 128), np.float32)
    for k in range(96):
        sel[k, k % 32, :] = 1.0
    return ident, maskf, maskb, sel.astype(ml_dtypes.bfloat16)


_POOL_IDX = [None, None]
_CACHE = {}


def build_program():
    nc = bass.Bass("TRN2", target_bir_lowering=False)
    din = {}

    def inp(name, shape, dt=F32):
        din[name] = nc.dram_tensor(name, list(shape), dt, kind="ExternalInput").ap()
        return din[name]

    xloc = inp("xloc", [SEQ, D])
    ctxl = inp("ctxl", [CTX, D])
    ccol = inp("ccol", [128, 16])
    w_ada = inp("w_ada", [D, 3 * D])
    b_ada = inp("b_ada", [1, 3 * D])
    nwcol = inp("nwcol", [128, 8])
    w_in = inp("w_in", [D, PROJ])
    w_dt = inp("w_dt", [D, 96])
    conv5 = inp("conv5", [128, 16, 5])
    convb = inp("convb", [128, 16])
    dtp = inp("dtp", [96, 4])
    dskc = inp("dskc", [128, 8])
    snwc = inp("snwc", [128, 8])
    pscc = inp("pscc", [128, 8])
    fnw = inp("fnw", [1, D])
    poolw = inp("poolw", [128, 8, 256])
    w_out = inp("w_out", [2 * D, D])
    c_ident = inp("c_ident", [128, 128])
    c_maskf = inp("c_maskf", [128, 128])
    c_maskb = inp("c_maskb", [128, 128])
    c_sel = inp("c_sel", [96, 32, 128], BF16)
    pm0, pidx0, pi0, iidx0 = _pool_tables(0)
    NM, NI = pm0.shape[1], pi0.shape[1]
    c_pm = inp("c_pm", [128, NM, 128], BF16)
    c_pi = inp("c_pi", [128, NI, 128])
    out_d = nc.dram_tensor("out", [TOWN, D], F32, kind="ExternalOutput").ap()
    ydram = nc.dram_tensor("ydram", [16, 128, TOWN], BF16, kind="Internal").ap()
    wbf = nc.dram_tensor("wbf", [D, PROJ], BF16, kind="Internal").ap()
    woutbf = nc.dram_tensor("woutbf", [2 * D, D], BF16, kind="Internal").ap()
    dbg_out = {k: nc.dram_tensor("dbg_" + k, list(v), F32, kind="ExternalOutput").ap() for k, v in DEBUG.items()}

    S = Sched(nc)

    def mm(out, lhsT, rhs, start=True, stop=True, tp=None):
        kw = {} if tp is None else {"tile_position": tp}
        S.add("pe", lambda e: e.matmul(out, lhsT=lhsT, rhs=rhs, start=start, stop=stop, **kw),
              reads=[lhsT, rhs] + ([] if start else [out]), writes=[out])

    def trp(out, in_, ident):
        S.add("pe", lambda e: e.transpose(out=out, in_=in_, identity=ident), reads=[in_, ident], writes=[out])

    def act(out, in_, func, bias=None, scale=None, accum=None, eng="act"):
        kw = {}
        if bias is not None:
            kw["bias"] = bias
        if scale is not None:
            kw["scale"] = scale
        if accum is not None:
            kw["accum_out"] = accum
        rd = [in_] + [a for a in (bias, scale) if a is not None and not isinstance(a, (int, float))]
        S.add(eng, lambda e: e.activation(out=out, in_=in_, func=func, **kw), reads=rd, writes=[out, accum])

    def tt(out, in0, in1, op, eng="dve"):
        S.add(eng, lambda e: e.tensor_tensor(out=out, in0=in0, in1=in1, op=op), reads=[in0, in1], writes=[out])

    def stt(out, in0, scalar, in1, op0, op1):
        rd = [in0, in1] + ([] if isinstance(scalar, (int, float)) else [scalar])
        S.add("dve", lambda e: e.scalar_tensor_tensor(out=out, in0=in0, scalar=scalar, in1=in1, op0=op0, op1=op1),
              reads=rd, writes=[out])

    def ts(out, in0, s1, s2, op0, op1=None, eng="dve"):
        rd = [in0] + [a for a in (s1, s2) if a is not None and not isinstance(a, (int, float))]
        if op1 is None:
            S.add(eng, lambda e: e.tensor_scalar(out=out, in0=in0, scalar1=s1, scalar2=None, op0=op0), reads=rd, writes=[out])
        else:
            S.add(eng, lambda e: e.tensor_scalar(out=out, in0=in0, scalar1=s1, scalar2=s2, op0=op0, op1=op1), reads=rd, writes=[out])

    def cp(out, in_, eng="dve"):
        S.add(eng, lambda e: e.tensor_copy(out=out, in_=in_), reads=[in_], writes=[out])

    def mset(ap, val, eng="pool"):
        S.add(eng, lambda e: e.memset(ap, val), writes=[ap])

    def dma(out, in_, eng="sp", after=()):
        S.add(eng, lambda e: e.dma_start(out=out, in_=in_), reads=[in_] + list(after), writes=[out], dma=True)

    def scan(out, d0, d1, init):
        rd = [d0, d1] + ([] if isinstance(init, (int, float)) else [init])
        S.add("dve", lambda e: e.tensor_tensor_scan(out=out, data0=d0, data1=d1, initial=init, op0=ALU.mult, op1=ALU.add),
              reads=rd, writes=[out])

    def recip(out, in_):
        S.add("dve", lambda e: e.reciprocal(out=out, in_=in_), reads=[in_], writes=[out])

    with ExitStack() as st:
        def sb(n, sh, dt=F32):
            return st.enter_context(nc.sbuf_tensor(n, list(sh), dt))

        def ps(n, sh, dt=F32):
            return st.enter_context(nc.psum_tensor(n, list(sh), dt))

        ident = sb("ident", [128, 128])
        identb = sb("identb", [128, 128], BF16)
        mask2 = sb("mask2", [128, 256])
        maskf = mask2[:, 0:128]
        maskb = mask2[:, 128:256]
        sel = sb("sel", [96, 32, 128], BF16)
        onesb = sb("onesb", [128, 128], BF16)
        ones32 = sb("ones32", [32, 128])
        epsc = sb("epsc", [128, 1])
        smallc = sb("smallc", [128, 80])
        ccs = sb("ccs", [128, 16])
        gcol = sb("gcol", [128, 32])
        gate_b = sb("gate_b", [128, D])
        conv5s = sb("conv5s", [128, 16, 5])
        dtps = sb("dtps", [96, 4])
        acol = sb("acol", [96, 1])
        xnT = sb("xnT", [128, 8 * XW], BF16)
        xnTv = xnT[:].rearrange("p (k t) -> p k t", k=8)
        woutv = xnT[:, 0:16 * D].rearrange("p (k n) -> p k n", k=16)
        Wg = sb("Wg", [128, 8, 768], BF16)
        Wdt = sb("Wdt", [128, 8, 96], BF16)
        pw = sb("pw", [128, 8, 256], BF16)
        rawseg = sb("rawseg", [128, 4 * 2180], BF16)
        rawv = rawseg[:].rearrange("p (c t) -> p c t", c=4)
        utm = rawseg[:, 0:20 * 256].rearrange("p (i c) -> p i c", i=20)
        xbct = sb("xbct", [128, 4 * 2176], BF16)
        xbcv = xbct[:].rearrange("p (c t) -> p c t", c=4)
        pmt = xbct[:, 0:NM * 128].rearrange("p (m t) -> p m t", m=NM)
        zsT = sb("zsT", [128, 2, TOWN], BF16)
        xstm = sb("xstm", [128, 17, 256], BF16)
        btm = sb("btm", [128, 17, 128], BF16)
        hsb = sb("hsb", [128, 16 * 256], BF16)
        hsbv = hsb[:].rearrange("p (c n) -> p c n", c=16)
        dTt = hsb[:, 0:1024].rearrange("p (c t) -> p c t", c=2)
        bwtm = sb("bwtm", [128, 17, 64])
        acum3 = sb("acum3", [96, TOWN], BF16)
        beta3 = sb("beta3", [96, TOWN], BF16)
        ebf = sb("ebf", [128, 2560], BF16)
        cdb = sb("cdb", [128, 17, 32])
        work = sb("work", [128, 4608])
        bfw = sb("bfw", [128, 4096], BF16)
        diag = bfw[:, 0:2560].rearrange("p (a b) -> p a b", a=20)
        fnw_b = work[:, 0:1024]
        pit = work[:, 1024:1024 + NI * 128].rearrange("p (a b) -> p a b", a=NI)
        hT = sb("hT", [128, 2, 256])
        hpart = sb("hpart", [128, 4, 256])
        hctx = sb("hctx", [128, 2, 4, 256])
        stage = sb("stage", [128, 3, D])
        pb = [ps("pb%d" % i, [128, 512]) for i in range(6)]
        pbb = [ps("pbb%d" % i, [128, 1024], BF16) for i in range(2)]

        dbg_cnt = [0]

        def dump(name, ap, shape=None):
            if name in dbg_out:
                dma(dbg_out[name], ap)

        def dump_cast(name, ap_bf, tmp_f32):
            if name in dbg_out:
                cp(tmp_f32, ap_bf, eng="pool")
                dma(dbg_out[name], tmp_f32)

        dma(ident[:], c_ident[:])
        dma(maskf, c_maskf[:])
        dma(maskb, c_maskb[:])
        dma(sel[:], c_sel[:])
        dma(conv5s[:], conv5[:])
        dma(dtps[:], dtp[:])
        dma(ccs[:], ccol[:])
        dma(smallc[:, 0:8], nwcol[:])
        dma(smallc[:, 8:24], convb[:])
        dma(smallc[:, 24:32], dskc[:])
        dma(smallc[:, 32:40], snwc[:])
        dma(smallc[:, 40:48], pscc[:])
        dma(Wdt[:], w_dt.rearrange("(k p) n -> p k n", p=128), eng="pool")
        def cast_weights(urgent, after=()):
            if urgent:
                cols = ((OFF_XBC, OFF_XBC + 1536),)
            else:
                cols = ((OFF_XBC + 1536, OFF_DT), (OFF_SZ, OFF_XBC), (0, OFF_SZ))
            for (c0, c1) in cols:
                for k in range(8):
                    dma(wbf[k * 128:(k + 1) * 128, c0:c1], w_in[k * 128:(k + 1) * 128, c0:c1], eng="pool", after=after)
            if not urgent:
                dma(pw[:], poolw[:], eng="pool", after=after)
                for k in range(16):
                    dma(woutbf[k * 128:(k + 1) * 128, :], w_out[k * 128:(k + 1) * 128, :], eng="pool", after=after)
        cp(identb[:], ident[:], eng="pool")
        mset(onesb[:], 1.0)
        mset(ones32[:], 1.0)
        mset(epsc[:], EPS)
        nw_c = smallc[:, 0:8]
        cb_c = smallc[:, 8:24]
        dsk_c = smallc[:, 24:32]
        snw_c = smallc[:, 32:40]
        psc_c = smallc[:, 40:48]
        act(acol[:], dtps[:, 1:2], AF.Exp)
        ts(acol[:], acol[:], -1.0, None, ALU.mult)

        def build_xnT(*a, **kw):
            for _ in build_xnT_gen(*a, **kw):
                pass

        def build_xnT_gen(src, ntok, col0, gbase, prologue_only=False, skip_prologue=False):
            nt = ntok // 128
            junk = bfw[:, 3072:4096]

            def stats(ti):
                xs_ = stage[:, ti % 3, :]
                dma(xs_, src[ti * 128:(ti + 1) * 128, :])
                ss = smallc[:, 48 + (ti % 3):49 + (ti % 3)]
                rs = smallc[:, 51 + (ti % 3):52 + (ti % 3)]
                act(junk, xs_, AF.Square, accum=ss)
                act(rs, ss, AF.Ln, bias=epsc[:], scale=1.0 / D)
                act(rs, rs, AF.Exp, scale=-0.5)
                ts(xs_, xs_, rs, None, ALU.mult)

            if not skip_prologue:
                stats(0)
                if nt > 1:
                    stats(1)
            if prologue_only:
                return
            for ti in range(nt):
                xs_ = stage[:, ti % 3, :]
                for k in range(8):
                    bank = pb[2 * (ti % 3) + k // 4]
                    trp(bank[:, (k % 4) * 128:(k % 4 + 1) * 128], xs_[:, k * 128:(k + 1) * 128], ident[:])
                for k in range(8):
                    bank = pb[2 * (ti % 3) + k // 4]
                    psrc = bank[:, (k % 4) * 128:(k % 4 + 1) * 128]
                    dst = xnTv[:, k, col0 + ti * 128:col0 + (ti + 1) * 128]
                    if k < 4:
                        act(dst, psrc, AF.Identity, bias=gcol[:, gbase + 8 + k:gbase + 9 + k], scale=gcol[:, gbase + k:gbase + k + 1])
                    else:
                        ts(dst, psrc, gcol[:, gbase + k:gbase + k + 1], gcol[:, gbase + 8 + k:gbase + 9 + k], ALU.mult, ALU.add)
                if ti + 2 < nt:
                    stats(ti + 2)
                yield ti

        build_xnT(ctxl, CTX, 2304, 16, prologue_only=True)
        S.mark('adaln')
        clb = work[:, 0:1024].rearrange("p (k m) -> p k m", k=8)
        ccb = work[:, 1024:2048].rearrange("p (k m) -> p k m", k=8)
        wstA = hctx[:].rearrange("p a b c -> p (a b) c")
        wstB = work[:, 2560:4608].rearrange("p (k n) -> p k n", k=8)
        bstA = hpart[:, 0, :]
        bstB = hpart[:, 1, :]
        modl = work[:, 2048:2048 + 256]
        modc = work[:, 2304:2304 + 256]
        for k in range(8):
            act(clb[:, k, :], ccs[:, k:k + 1].broadcast_to([128, 128]), AF.Silu)
            act(ccb[:, k, 0:64], ccs[:, k:k + 1].broadcast_to([128, 64]), AF.Silu)
            act(ccb[:, k, 64:128], ccs[:, 8 + k:9 + k].broadcast_to([128, 64]), AF.Silu)
        for nb in range(12):
            c0 = nb * 256
            wst = wstA if nb % 2 == 0 else wstB
            bst = bstA if nb % 2 == 0 else bstB
            if nb == 6:
                cast_weights(True)
            dma(wst, w_ada[:, c0:c0 + 256].rearrange("(k p) n -> p k n", p=128))
            dma(bst, b_ada[0:1, c0:c0 + 256].broadcast_to([128, 256]))
            lhs = ccb if nb < 8 else clb
            pacc = pb[nb % 2]
            for k in range(8):
                mm(pacc[:, 0:256], lhs[:, k, :], wst[:, k, :], start=(k == 0), stop=(k == 7))
            tt(modl, pacc[:, 0:256], bst, ALU.add)
            if nb >= 8:
                cp(gate_b[:, c0 - 2048:c0 - 2048 + 256], modl, eng="pool")
            else:
                pbank = pb[2 + (nb % 2)]
                fk0 = (nb % 4) * 2
                pick = ident[:, 0:128].rearrange("p (a b) -> p a b", b=64)[:, :, 0]
                for hh in range(2):
                    mm(pbank[:, 2 * hh:2 * hh + 2], modl[:, hh * 128:(hh + 1) * 128], pick, start=(hh == 0), stop=True)
                for which in range(2):
                    base = which * 16
                    pair = pbank[:, 0:4].rearrange("p (h w) -> p h w", w=2)[:, :, which]
                    if nb < 4:
                        cp(gcol[:, base + 8 + fk0:base + 10 + fk0], pair)
                    else:
                        stt(gcol[:, base + fk0:base + fk0 + 2], pair, 1.0, nw_c[:, fk0:fk0 + 2], ALU.add, ALU.mult)
        dump("gcol", gcol[:])
        dump("gate_b", gate_b[:])

        def load_w(dst, col_list):
            o = 0
            for c0, n in col_list:
                dma(dst[:, :, o:o + n], wbf[:, c0:c0 + n].rearrange("(k p) n -> p k n", p=128))
                o += n

        pbsel = [0]

        def proj_fm(wt, wc0, m, tok0, n):
            bank = pb[pbsel[0] % 2]
            pbsel[0] += 1
            for k in range(8):
                mm(bank[0:m, 0:n], wt[:, k, wc0:wc0 + m], xnTv[:, k, tok0:tok0 + n], start=(k == 0), stop=(k == 7))
            return bank[0:m, 0:n]

        def blocks(n0, n1):
            r = []
            t = n0
            while t < n1:
                r.append((t, min(512, n1 - t)))
                t += 512
            return r

        def wv(i):
            return work[0:96, i * 512:(i + 1) * 512]

        def dt_path(tok0, ntok, mode, tile0, bw_of=None):
            car = None
            for car in dt_path_gen(tok0, ntok, mode, tile0, bw_of=bw_of):
                pass
            return car

        def interleave(*gens):
            gens = list(gens)
            last = [None] * len(gens)
            alive = [True] * len(gens)
            while any(alive):
                for i, gn in enumerate(gens):
                    if alive[i]:
                        try:
                            last[i] = next(gn)
                        except StopIteration:
                            alive[i] = False
            return last

        def dt_path_gen(tok0, ntok, mode, tile0, bw_of=None):
            carry = None
            ones_m_full = wv(8)
            mset(ones_m_full, 1.0, eng="dve")
            if mode == "own":
                for c in range(4):
                    mset(wv(8)[:, c * 128:c * 128 + 1], 0.0, eng="dve")
            for (t0, n) in blocks(tok0, tok0 + ntok):
                bi = (t0 - tok0) // 512
                dtv, adt, P, cum, tmp, lnd, A, B_ = [wv(i)[:, 0:n] for i in range(8)]
                pp = proj_fm(Wdt, 0, 96, t0, n)
                ts(tmp, pp, dtps[:, 0:1], None, ALU.add)
                stt(A, tmp, -1.0, tmp, ALU.mult, ALU.max)
                act(A, A, AF.Exp, scale=-1.0)
                act(A, A, AF.Ln, bias=1.0)
                stt(dtv, tmp, 0.0, A, ALU.max, ALU.add)
                ts(adt, dtv, acol[:, 0:1], None, ALU.mult)
                act(lnd, dtv, AF.Ln)
                if mode == "own":
                    ones_m = wv(8)[:, 0:n]
                    scan(P, ones_m, adt, 0.0)
                    nch = n // 128
                    P3 = P.rearrange("p (c l) -> p c l", l=128)
                    totb = P3[:, :, 127:128].broadcast_to([96, nch, 128])
                    tt(tmp.rearrange("p (c l) -> p c l", l=128), totb, P3, ALU.subtract)
                    tt(tmp, tmp, adt, ALU.add)
                    ts(cum, P, dtps[:, 2:3], None, ALU.mult)
                    stt(cum, tmp, dtps[:, 3:4], cum, ALU.mult, ALU.add)
                    tt(A, lnd, cum, ALU.subtract)
                    tt(tmp.rearrange("p (c l) -> p c l", l=128), totb, cum.rearrange("p (c l) -> p c l", l=128), ALU.subtract)
                    act(tmp, tmp, AF.Exp)
                    tt(B_, dtv, tmp, ALU.mult)
                    cp(A[32:64, :], B_[32:64, :])
                    hi = ebf[0:96, 0:n]
                    md = ebf[0:96, 512:512 + n]
                    cp(hi, cum)
                    tt(tmp, cum, hi, ALU.subtract)
                    cp(md, tmp)
                    tt(lnd, tmp, md, ALU.subtract)
                    a0 = t0 - tok0
                    cp(acum3[0:32, a0:a0 + n], hi[0:32, :])
                    cp(acum3[32:64, a0:a0 + n], md[32:64, :])
                    cp(acum3[64:96, a0:a0 + n], lnd[64:96, :])
                    cdv = hctx[0:32, 1, 0, 0:nch]
                    act(cdv, P3[0:32, :, 127], AF.Exp)
                    for c in range(nch):
                        dg = hctx[0:32, 1, 1, c * 32:(c + 1) * 32]
                        ts(dg, ident[0:32, 0:32], cdv[:, c:c + 1], None, ALU.mult)
                        mm(pb[5][:, c * 32:(c + 1) * 32], ones32[:, :], dg, start=(c == 0), stop=True)
                    cp(cdb[:, tile0 + bi * 4:tile0 + bi * 4 + nch, :], pb[5][:, 0:nch * 32].rearrange("p (c k) -> p c k", k=32))
                else:
                    ones_m = wv(8)[:, 0:n]
                    scan(P, ones_m, adt, 0.0 if carry is None else carry)
                    carry = smallc[0:96, 60 + (bi % 2):61 + (bi % 2)]
                    cp(carry, P[:, n - 1:n])
                    tt(tmp, P, adt, ALU.subtract)
                    act(tmp, tmp, AF.Exp)
                    tt(A, dtv, tmp, ALU.mult)
                    if mode == "ctx":
                        ts(tmp, P, P[:, n - 1:n], -1.0, ALU.subtract, ALU.mult)
                        act(tmp, tmp, AF.Exp)
                        tt(B_, dtv, tmp, ALU.mult)
                        ts(A, A, dtps[:, 3:4], None, ALU.mult)
                        stt(A, B_, dtps[:, 2:3], A, ALU.mult, ALU.add)
                nt_ = n // 128
                for c in range(nt_):
                    trp(pb[5][:, 128 + c * 64:192 + c * 64], A[0:64, c * 128:(c + 1) * 128], ident[0:64, 0:64])
                if bw_of is None:
                    cp(bwtm[:, tile0 + bi * 4:tile0 + bi * 4 + nt_, :], pb[5][:, 128:128 + nt_ * 64].rearrange("p (c k) -> p c k", k=64))
                else:
                    for c in range(nt_):
                        cp(bw_of(bi * 4 + c), pb[5][:, 128 + c * 64:192 + c * 64])
                yield carry

        def conv_stage(*a, **kw):
            for _ in conv_stage_gen(*a, **kw):
                pass

        def conv_stage_gen(g, wt, ncol_chunks, tok0, ntok, zero_left, zero_right, with_c, after_proj=None,
                           raw_of=None, xbc_of=None, build_diag=True, conv_desc=False, n_out=None):
            nchk = 4 if with_c else 3
            if raw_of is None:
                raw_of = lambda ci, a, b: rawv[:, ci, a:b]
            if xbc_of is None:
                xbc_of = lambda ci, a, b: xbcv[:, ci, a:b]
            chs = [2 * g, 2 * g + 1, 8 + g, 12 + g][:nchk]
            if build_diag:
                for ci, ch in enumerate(chs):
                    for k in range(5):
                        ts(diag[:, ci * 5 + k, :], identb[:], conv5s[:, ch, k:k + 1], None, ALU.mult)
            for ci in range(nchk):
                if zero_left:
                    mset(raw_of(ci, 0, 2), 0.0)
                if zero_right:
                    mset(raw_of(ci, 2 + ntok, 4 + ntok), 0.0)
            for (t0, n) in blocks(tok0, tok0 + ntok):
                for ci in range(nchk):
                    pp = proj_fm(wt, ci * 128, 128, t0, n)
                    o0 = 2 + t0 - tok0
                    if ci % 2 == 0:
                        act(raw_of(ci, o0, o0 + n), pp, AF.Identity)
                    else:
                        cp(raw_of(ci, o0, o0 + n), pp)
                yield None
            if after_proj is not None:
                after_proj()
            cblocks = blocks(0, ntok if n_out is None else n_out)
            for (t0, n) in (cblocks[::-1] if conv_desc else cblocks):
                for ci, ch in enumerate(chs):
                    bank = pb[2 + (ci % 2)]
                    for k in range(5):
                        mm(bank[:, 0:n], diag[:, ci * 5 + k, :], raw_of(ci, t0 + k, t0 + k + n), start=(k == 0), stop=(k == 4))
                    act(xbc_of(ci, t0, t0 + n), bank[:, 0:n], AF.Silu, bias=cb_c[:, ch:ch + 1])
                yield None

        def to_tokmajor(ntiles, t_off, xbc_of=None, xs_of=None, b_of=None, first=0, tiles=None):
            if xbc_of is None:
                xbc_of = lambda ci, a, b: xbcv[:, ci, a:b]
            if xs_of is None:
                xs_of = lambda i: xstm[:, i, :]
            if b_of is None:
                b_of = lambda i: btm[:, i, :]
            for i in (range(first, ntiles) if tiles is None else tiles):
                bank = pbb[i % 2]
                for ci in range(3):
                    trp(bank[:, ci * 128:(ci + 1) * 128], xbc_of(ci, t_off + i * 128, t_off + (i + 1) * 128), identb[:])
                if i % 2 == 0:
                    cp(xs_of(i), bank[:, 0:256])
                    act(b_of(i), bank[:, 256:384], AF.Identity)
                else:
                    act(xs_of(i), bank[:, 0:256], AF.Identity)
                    cp(b_of(i), bank[:, 256:384])

        def xdd_of(i, wcol0, g, dst, bw=None, xs=None):
            bw = bwtm[:, i, :] if bw is None else bw
            xs = xstm[:, i, :] if xs is None else xs
            wq = bw[:, wcol0 + 4 * g:wcol0 + 4 * g + 4].unsqueeze(2).broadcast_to([128, 4, 64])
            tt(dst.rearrange("p (h q) -> p h q", h=4), xs.rearrange("p (h q) -> p h q", h=4), wq, ALU.mult)

        xdd0 = bfw[:, 2048:2304]
        xdd1 = bfw[:, 2304:2560]
        def w_xb(g):
            return [(OFF_XBC + 256 * g, 256), (OFF_XBC + 1024 + 128 * g, 128)]

        def w_ssd(g):
            return w_xb(g) + [(OFF_XBC + 1536 + 128 * g, 128), (OFF_SZ + 256 * g, 256)]

        CX0 = 2304
        c_raw = lambda ci, a, b: rawv[:, 3, ci * 264 + a:ci * 264 + b]
        c_xbc = lambda ci, a, b: xbcv[:, 3, ci * 256 + a:ci * 256 + b]
        c_xs = lambda i: hsb[:, i * 256:(i + 1) * 256]
        c_b = lambda i: hsb[:, 512 + i * 128:512 + (i + 1) * 128]
        c_bw = lambda i: stage[:, 2, i * 64:(i + 1) * 64]
        build_xnT(ctxl, CTX, CX0, 16, skip_prologue=True)
        S.mark('partner')
        def staged(gx, gd, gc, ntiles, dt_after, cv_after):
            car = None
            for ti in range(ntiles):
                next(gx)
                if ti in dt_after:
                    car = next(gd)
                if ti in cv_after:
                    next(gc)
            for _ in gx:
                pass
            for car2 in gd:
                car = car2
            for _ in gc:
                pass
            return car

        gx = build_xnT_gen(xloc[1920:4096, :], 2176, 0, 0)
        next(gx)
        next(gx)
        load_w(Wg, w_xb(0))
        gen0 = conv_stage_gen(0, Wg, 3, 126, 2050, False, True, False)
        car = staged(gx, dt_path_gen(128, TOWN, "part", 1), gen0, 15, (4, 8, 12), (3, 7, 11))
        dt_path(CX0, CTX, "ctx", 0, bw_of=c_bw)
        cast_weights(False, after=[xnTv[:, 7, 2048:2176]])
        cdv = work[0:32, 4096:4097]
        act(cdv, car[0:32, :], AF.Exp)
        dgp = work[0:32, 4160:4192]
        ts(dgp, ident[0:32, 0:32], cdv, None, ALU.mult)
        mm(pb[5][:, 0:32], ones32[:, :], dgp, start=True, stop=True)
        cp(cdb[:, 0, :], pb[5][:, 0:32])
        build_xnT(xloc[0:XW, :], XW, 0, 0, prologue_only=True)
        for g in range(4):
            gc = conv_stage_gen(g, Wg, 3, CX0, CTX, True, True, False, raw_of=c_raw, xbc_of=c_xbc, build_diag=False)
            nxt = w_xb(g + 1) if g < 3 else w_ssd(0)
            if g > 0:
                gp = conv_stage_gen(g, Wg, 3, 126, 2050, False, True, False)
                for _ in range(5):
                    next(gp)
                next(gc)
                load_w(Wg, nxt)
                for _ in gp:
                    pass
                for _ in gc:
                    pass
            else:
                next(gc)
                load_w(Wg, nxt)
                for _ in gc:
                    pass
            to_tokmajor(17, -126, first=1)
            for i in range(1, 17):
                xd = xdd0 if i % 2 == 0 else xdd1
                xdd_of(i, 32 + 16, g, xd)
                mm(pb[4][:, 0:256], btm[:, i, :], xd, start=(i == 1), stop=(i == 16))
            to_tokmajor(2, 0, xbc_of=c_xbc, xs_of=c_xs, b_of=c_b)
            for d in range(2):
                for i in range(2):
                    xd = xdd0 if i % 2 == 0 else xdd1
                    xdd_of(i, 32 + 16 * d, g, xd, bw=c_bw(i), xs=c_xs(i))
                    mm(pb[5][:, 256:512], c_b(i), xd, start=(i == 0), stop=(i == 1))
                cp(hctx[:, d, g, :], pb[5][:, 256:512])
            tt(hpart[:, g, :].rearrange("p (h q) -> p h q", h=4), hctx[:, 1, g, :].rearrange("p (h q) -> p h q", h=4),
               cdb[:, 0, 16 + 4 * g:16 + 4 * g + 4].unsqueeze(2).broadcast_to([128, 4, 64]), ALU.mult)
            tt(hpart[:, g, :], hpart[:, g, :], pb[4][:, 0:256], ALU.add)
        dump("hpart", hpart[:].rearrange("p a b -> p (a b)"))

        S.mark('own_xn_dt')
        gxo = build_xnT_gen(xloc[0:XW, :], XW, 0, 0, skip_prologue=True)
        dump_cast("xnT", xnTv[:, 0, 0:512], work[:, 0:512])
        staged(gxo, dt_path_gen(0, TOWN, "own", 0), conv_stage_gen(0, Wg, 4, 0, TOWN + 2, True, False, True, n_out=TOWN),
               20, (4, 8, 12, 16), (5, 9, 13, 17))
        dump("bwtm", bwtm[:, 0:16, :].rearrange("p a b -> p (a b)"))
        dump("cdb", cdb[:, 0:16, :].rearrange("p a b -> p (a b)"))

        Ef = ebf[:, 0:512]
        Eb = ebf[:, 512:1024]
        EAf = ebf[:, 1024:1536]
        EAb = ebf[:, 1536:2048]
        rstd = work[:, 3584:4096]
        Gf = bfw[:, 0:512]
        Gb = bfw[:, 512:1024]
        Qf = bfw[:, 1024:1536]
        Qb = bfw[:, 1536:2048]
        hTb = bfw[:, 2560:2816]
        sqb = bfw[:, 3072:4096].rearrange("p (a t) -> p a t", a=2)

        cbms = [(ebf[:, 2048:2176], ebf[:, 2176:2304]), (ebf[:, 2304:2432], ebf[:, 2432:2560])]
        dirbuf = ((pb[2], Ef, EAf, Gf, Qf), (pb[3], Eb, EAb, Gb, Qb))

        def w_pool(g):
            return [(256 * g, 256), (OFF_PZ + 256 * g, 256)]

        for g in range(4):
            S.mark('ssd_g%d' % g)
            def pre_s(c, slot):
                xd = bfw[:, 2560:2816] if c % 2 == 0 else bfw[:, 2816:3072]
                xdd_of(c, 32 + 16, g, xd)
                bank = pb[4 + slot // 2]
                mm(bank[:, (slot % 2) * 256:(slot % 2 + 1) * 256], btm[:, c, :], xd, start=(slot % 2 == 0), stop=True)

            def pre_rec(c, slot):
                bank = pb[4 + slot // 2]
                cp(hsbv[:, c, :], hT[:, 1, :])
                tt(hT[:, 1, :].rearrange("p (h q) -> p h q", h=4), hT[:, 1, :].rearrange("p (h q) -> p h q", h=4),
                   cdb[:, c, 16 + 4 * g:16 + 4 * g + 4].unsqueeze(2).broadcast_to([128, 4, 64]), ALU.mult)
                tt(hT[:, 1, :], hT[:, 1, :], bank[:, (slot % 2) * 256:(slot % 2 + 1) * 256], ALU.add)

            zunits = [(t0, n, cc) for (t0, n) in blocks(0, TOWN) for cc in range(2)]

            def zunit(u):
                t0, n, cc = zunits[u]
                pp = proj_fm(Wg, 512 + cc * 128, 128, t0, n)
                act(zsT[:, cc, t0:t0 + n], pp, AF.Silu)

            S.mark('ssd_g%d_pre' % g)
            cp(hT[:, 1, :], hpart[:, g, :])
            if g > 0:
                gp = conv_stage_gen(g, Wg, 4, 0, TOWN + 2, True, False, True, conv_desc=True, n_out=TOWN)
                for _ in range(5):
                    next(gp)
                blist = (3, 2, 1, 0)
            else:
                blist = (3, 2, 1, 0)
                to_tokmajor(16, 0)
            for bblk in blist:
                if g > 0:
                    next(gp)
                    to_tokmajor(16, 0, tiles=[4 * bblk + 3, 4 * bblk + 2, 4 * bblk + 1, 4 * bblk])
                chs = [4 * bblk + 3, 4 * bblk + 2, 4 * bblk + 1, 4 * bblk]
                for slot, c in enumerate(chs):
                    pre_s(c, slot)
                zunit(2 * bblk)
                zunit(2 * bblk + 1)
                for slot, c in enumerate(chs):
                    pre_rec(c, slot)
            if g > 0:
                for _ in gp:
                    pass
            load_w(Wg, w_ssd(g + 1) if g < 3 else [(0, 512), (OFF_PZ, 256)])
            S.mark('ssd_g%d_main' % g)
            cp(hT[:, 0, :], hctx[:, 0, g, :])
            cp(hTb, hctx[:, 0, g, :])

            def tks(c):
                return slice(c * 128, (c + 1) * 128)

            ygs = [work[:, 2560:3584].rearrange("p (a t) -> p a t", a=2), work[:, 0:1024].rearrange("p (a t) -> p a t", a=2)]
            ytmp2 = work[:, 4096:4352].rearrange("p (a t) -> p a t", a=2)

            def cb_front(c):
                tk = tks(c)
                mm(pb[1][:, 0:128], xbcv[:, 2, tk], xbcv[:, 3, tk], start=True, stop=True)

            def cbm_of(c):
                base = 2048 + 256 * (c % 2)
                tt(ebf[:, base:base + 256].rearrange("p (d l) -> p d l", d=2),
                   pb[1][:, 0:128].unsqueeze(1).broadcast_to([128, 2, 128]),
                   mask2[:, :].rearrange("p (d l) -> p d l", d=2), ALU.mult)

            def ps_front(c):
                tk = tks(c)
                for d in range(2):
                    PS = dirbuf[d][0]
                    for h in range(4):
                        dh = 16 * d + 4 * g + h
                        mm(PS[:, h * 128:(h + 1) * 128], sel[:, dh, :], acum3[:, tk], start=(h == 0), stop=True)

            def ea_of(c, d):
                PS, E, EA, G, Q = dirbuf[d]
                act(EA, PS[:, :], AF.Exp)

            def bmm_of(c, d):
                pass

            def e_of(c, d):
                PS, E, EA, G, Q = dirbuf[d]
                for h in range(4):
                    dh = 16 * d + 4 * g + h
                    act(E[:, h * 128:(h + 1) * 128], PS[:, h * 128:(h + 1) * 128], AF.Exp, bias=bwtm[:, c, dh:dh + 1])

            def g_only(c, d):
                PS, E, EA, G, Q = dirbuf[d]
                cbm = cbms[c % 2][d]
                stt(G.rearrange("p (h l) -> p h l", h=4), E.rearrange("p (h l) -> p h l", h=4), BIG,
                    cbm.unsqueeze(1).broadcast_to([128, 4, 128]), ALU.min, ALU.mult)

            def q_both(c):
                tt(bfw[:, 1024:2048].rearrange("p (h l) -> p h l", h=8), ebf[:, 1024:2048].rearrange("p (h l) -> p h l", h=8),
                   xbcv[:, 3, tks(c)].unsqueeze(1).broadcast_to([128, 8, 128]), ALU.mult)

            def norm_sq(blk):
                yg = ygs[blk % 2]
                for hp in range(2):
                    act(sqb[:, hp, :], yg[:, hp, :], AF.Square)

            def norm_mm(blk):
                for hp in range(2):
                    mm(pb[0][:, :], onesb[:], sqb[:, hp, :], start=(hp == 0), stop=(hp == 1))

            def norm_rs(blk):
                act(rstd, pb[0][:, :], AF.Ln, bias=epsc[:], scale=1.0 / 256.0)
                act(rstd, rstd, AF.Exp, scale=-0.5)

            def norm_out(blk):
                yg = ygs[blk % 2]
                for hp in range(2):
                    pr = 2 * g + hp
                    ys = sqb[:, hp, :]
                    stt(ys, yg[:, hp, :], snw_c[:, pr:pr + 1], rstd, ALU.mult, ALU.mult)
                    dma(ydram[8 + pr, :, blk * 512:(blk + 1) * 512], ys, eng="pool")

            def ys_bank(c):
                return pb[4 + (c % 2)]

            def upd(c):
                tt(hT[:, 0, :].rearrange("p (h q) -> p h q", h=4), hT[:, 0, :].rearrange("p (h q) -> p h q", h=4),
                   cdb[:, c, 4 * g:4 * g + 4].unsqueeze(2).broadcast_to([128, 4, 64]), ALU.mult)
                tt(hT[:, 0, :], hT[:, 0, :], ys_bank(c)[:, 256:512], ALU.add)
                cp(hTb, hT[:, 0, :])

            def ymm(c):
                bank = ys_bank(c)
                for hp in range(2):
                    Y = bank[:, hp * 128:(hp + 1) * 128]
                    for hh in range(2):
                        h = 2 * hp + hh
                        o = Y[hh * 64:(hh + 1) * 64, :]
                        tp = (0, 64 * hh)
                        xs_h = xstm[:, c, h * 64:(h + 1) * 64]
                        mm(o, xs_h, Gf[:, h * 128:(h + 1) * 128], start=(hp == 0), stop=False, tp=tp)
                        mm(o, hTb[:, h * 64:(h + 1) * 64], Qf[:, h * 128:(h + 1) * 128], start=False, stop=False, tp=tp)
                        mm(o, xs_h, Gb[:, h * 128:(h + 1) * 128], start=False, stop=False, tp=tp)
                        mm(o, hsbv[:, c, h * 64:(h + 1) * 64], Qb[:, h * 128:(h + 1) * 128], start=False, stop=True, tp=tp)
                xd = xdd0 if c % 2 == 0 else xdd1
                mm(bank[:, 256:512], btm[:, c, :], xd, start=False, stop=True)

            def evac(c):
                tk = tks(c)
                yg = ygs[(c // 4) % 2]
                bank = ys_bank(c)
                for hp in range(2):
                    pr = 2 * g + hp
                    stt(ytmp2[:, hp, :], xbcv[:, hp, tk], dsk_c[:, pr:pr + 1], bank[:, hp * 128:(hp + 1) * 128], ALU.mult, ALU.add)
                tt(yg[:, :, (c % 4) * 128:(c % 4 + 1) * 128], ytmp2, zsT[:, :, tk], ALU.mult)

            cb_front(0)
            cbm_of(0)
            cb_front(1)
            cbm_of(1)
            ps_front(0)
            for d in range(2):
                ea_of(0, d)
                bmm_of(0, d)
                e_of(0, d)
            for c in range(17):
                pend = ((c - 1) // 4 - 1) if ((c - 1) % 4 == 0 and c > 1) else None
                if c < 16:
                    if c + 2 < 16:
                        cb_front(c + 2)
                    if c + 1 < 16:
                        ps_front(c + 1)
                if pend is not None:
                    norm_mm(pend)
                if c < 16:
                    g_only(c, 0)
                    if c + 1 < 16:
                        e_of(c + 1, 0)
                    g_only(c, 1)
                    q_both(c)
                    if c + 1 < 16:
                        ea_of(c + 1, 0)
                        ea_of(c + 1, 1)
                if pend is not None:
                    norm_rs(pend)
                if c < 16:
                    xdd_of(c, 32, g, xdd0 if c % 2 == 0 else xdd1)
                if 1 <= c < 16:
                    upd(c - 1)
                if c < 16:
                    ymm(c)
                    if c + 1 < 16:
                        bmm_of(c + 1, 1)
                        e_of(c + 1, 1)
                    if c + 2 < 16:
                        cbm_of(c + 2)
                if c >= 1:
                    evac(c - 1)
                    if (c - 1) % 4 == 3:
                        norm_sq((c - 1) // 4)
                if pend is not None:
                    norm_out(pend)
            norm_mm(3)
            norm_rs(3)
            norm_out(3)

        S.mark('pool')
        dma(pmt, c_pm[:])
        dma(pit, c_pi[:])
        midx, iidx = _POOL_IDX
        utmA = utm
        utmB = xbct[:, NM * 128:NM * 128 + 20 * 256].rearrange("p (i c) -> p i c", i=20)

        def w_pool_pair(gp):
            return [(256 * gp, 512), (OFF_PZ + 256 * gp, 256)]

        def pool_z(g):
            for (t0, n) in blocks(0, TOWN):
                for cc in range(2):
                    pp = proj_fm(Wg, 512 + cc * 128, 128, t0, n)
                    act(zsT[:, cc, t0:t0 + n], pp, AF.Silu)

        def pool_body(g, utm):
            for blk in range(4):
                for cc in range(2):
                    pbank = pb[2 * (blk % 2) + cc]
                    for jj in range(4):
                        j = blk * 4 + jj
                        ds = sorted(d for (gg, j2, d) in midx if gg == g and j2 == j)
                        o = pbank[:, jj * 128:(jj + 1) * 128]
                        for di, d in enumerate(ds):
                            mm(o, utm[:, j + d, cc * 128:(cc + 1) * 128], pmt[:, midx[(g, j, d)], :],
                               start=(di == 0 and jj == 0), stop=(di == len(ds) - 1))
                    ii = [iidx[(g, blk * 4 + jj)] for jj in range(4)]
                    d3 = dTt[:, cc, :].rearrange("p (j t) -> p j t", j=4)
                    p3 = pbank[:, :].rearrange("p (j t) -> p j t", j=4)
                    if len(set(ii)) == 1:
                        tt(d3, p3, pit[:, ii[0]:ii[0] + 1, :].broadcast_to([128, 4, 128]), ALU.mult)
                    elif ii == list(range(ii[0], ii[0] + 4)):
                        tt(d3, p3, pit[:, ii[0]:ii[0] + 4, :], ALU.mult)
                    else:
                        for jj in range(4):
                            tt(d3[:, jj, :], p3[:, jj, :], pit[:, ii[jj], :], ALU.mult)
                t0 = blk * 512
                for oc in range(2):
                    lb = pb[4 + oc]
                    for cc in range(2):
                        mm(lb[:, :], pw[:, 2 * g + cc, oc * 128:(oc + 1) * 128], dTt[:, cc, :], start=(cc == 0), stop=(cc == 1))
                    yp = bfw[:, (2 * (blk % 2) + oc) * 512:(2 * (blk % 2) + oc + 1) * 512]
                    pr = 2 * g + oc
                    stt(yp, lb[:, :], psc_c[:, pr:pr + 1], zsT[:, oc, t0:t0 + 512], ALU.mult, ALU.mult)
                    dma(ydram[pr, :, t0:t0 + 512], yp, eng="pool")


        for gp in (0, 2):
            for i in range(17 if gp == 0 else 20):
                bank = pb[i % 2]
                for k in range(8):
                    mm(bank[:, 0:512], xnTv[:, k, i * 128:(i + 1) * 128], Wg[:, k, 0:512], start=(k == 0), stop=(k == 7))
                cp(utmA[:, i, :], bank[:, 0:256])
                act(utmB[:, i, :], bank[:, 256:512], AF.Identity)
            pool_z(gp)
            load_w(Wg[:, :, 512:768], [(OFF_PZ + 256 * (gp + 1), 256)])
            pool_body(gp, utmA)
            pool_z(gp + 1)
            if gp == 0:
                load_w(Wg, w_pool_pair(2))
            pool_body(gp + 1, utmB)

        S.mark('outproj')
        dma(woutv, woutbf.rearrange("(k p) n -> p k n", p=128))
        dma(fnw_b, fnw[0:1, :].broadcast_to([128, D]))
        ystA = rawseg[:, 0:16 * 512].rearrange("p (k t) -> p k t", k=16)
        ystB = xbct[:, 0:16 * 512].rearrange("p (k t) -> p k t", k=16)
        def ld_yst(blk):
            dma(ystA if blk % 2 == 0 else ystB, ydram[:, :, blk * 512:(blk + 1) * 512].rearrange("k p t -> p k t"))

        def ld_x(i):
            dma(stage[:, i % 2, :], xloc[i * 128:(i + 1) * 128, :])

        ld_yst(0)
        ld_x(0)
        for blk in range(4):
            yst = ystA if blk % 2 == 0 else ystB
            if blk + 1 < 4:
                ld_yst(blk + 1)
            for jj in range(4):
                i = blk * 4 + jj
                if i + 1 < 16:
                    ld_x(i + 1)
                xs_ = stage[:, i % 2, :]
                hh_ = work[:, 1024 + (i % 2) * 1024:2048 + (i % 2) * 1024]
                oo_ = stage[:, 2, :] if i % 2 == 0 else work[:, 3072:4096]
                pbo = 2 * (i % 2)
                for nb in range(2):
                    for k in range(16):
                        mm(pb[pbo + nb][:, :], yst[:, k, jj * 128:(jj + 1) * 128], woutv[:, k, nb * 512:(nb + 1) * 512],
                           start=(k == 0), stop=(k == 15))
                for nb in range(2):
                    sl = slice(nb * 512, (nb + 1) * 512)
                    tt(hh_[:, sl], pb[pbo + nb][:, :], gate_b[:, sl], ALU.mult)
                tt(hh_, hh_, xs_, ALU.add)
                ss = smallc[:, 56 + 2 * (i % 2):57 + 2 * (i % 2)]
                rs = smallc[:, 57 + 2 * (i % 2):58 + 2 * (i % 2)]
                act(oo_, hh_, AF.Square, accum=ss)
                act(rs, ss, AF.Ln, bias=epsc[:], scale=1.0 / D)
                act(rs, rs, AF.Exp, scale=-0.5)
                stt(oo_, hh_, rs, fnw_b, ALU.mult, ALU.mult)
                dma(out_d[i * 128:(i + 1) * 128, :], oo_, eng="pool")
        S.emit()
    _CACHE['sched'] = S
    return nc


def _prep_core(b, half, x, c, ctx, c_ctx, norm_w, w_ada, b_ada, w_in, conv_w, conv_b, a_log, dt_bias,
               d_skip, ssd_norm_w, pool_w, pool_scale, w_out, final_norm_w, consts):
    f32 = np.float32

    def col8(v):
        return np.ascontiguousarray(v.reshape(-1, 128).T).astype(f32)

    xl = x[b][::-1] if half else x[b]
    cl = ctx[b][::-1] if half else ctx[b]
    W = w_in[0]
    dl = [half, 1 - half]
    wdt = np.concatenate([W[:, OFF_DT + 16 * dl[0]:OFF_DT + 16 * dl[0] + 16],
                          W[:, OFF_DT + 16 * dl[1]:OFF_DT + 16 * dl[1] + 16]], 1)
    wdt = np.concatenate([wdt, wdt, wdt], 1)
    cw = conv_w[0]
    z = np.zeros_like(cw[0])
    taps = [cw[0], cw[1], cw[2], cw[3], z] if half == 0 else [z, cw[3], cw[2], cw[1], cw[0]]
    conv5 = np.stack([t.reshape(16, 128).T for t in taps], 2)
    dtb = np.concatenate([dt_bias[0, dl[0]], dt_bias[0, dl[1]]])
    alg = np.concatenate([a_log[0, dl[0]], a_log[0, dl[1]]])
    mf = np.concatenate([np.ones(16), np.zeros(16)])
    dtp = np.stack([np.tile(dtb, 3), np.tile(alg, 3), np.tile(mf, 3), np.tile(1 - mf, 3)], 1)
    dsk = np.repeat(d_skip[0].reshape(8, 2, 1), 64, 2).reshape(8, 128).T
    ident, maskf, maskb, sel = consts
    pm, midx, pi, iidx = _pool_tables(half)
    pwl = pool_w[0].reshape(4, 2, 128, 256).transpose(2, 0, 1, 3).reshape(128, 8, 256)
    m = {
        "xloc": xl, "ctxl": cl,
        "ccol": np.concatenate([col8(c[b]), col8(c_ctx)], 1),
        "w_ada": w_ada[0], "b_ada": b_ada[0][None, :],
        "nwcol": col8(norm_w[0]), "w_in": W, "w_dt": wdt,
        "conv5": conv5, "convb": col8(conv_b[0]), "dtp": dtp, "dskc": dsk,
        "snwc": col8(ssd_norm_w[0]), "pscc": col8(pool_scale[0]), "fnw": final_norm_w[None, :],
        "poolw": pwl, "w_out": w_out[0],
        "c_ident": ident, "c_maskf": maskf, "c_maskb": maskb, "c_sel": sel, "c_pm": pm, "c_pi": pi,
    }
    out = {}
    for k, v in m.items():
        if v.dtype == ml_dtypes.bfloat16:
            out[k] = np.ascontiguousarray(v)
        else:
            out[k] = np.ascontiguousarray(v, dtype=f32)
    return out, midx, iidx


def kernel(**inputs):
    inputs = {k: np.asarray(v) for k, v in inputs.items()}
    consts = _consts()
    in_maps = []
    idxs = None
    for core in range(8):
        b, half = core // 2, core % 2
        m, midx, iidx = _prep_core(b, half, consts=consts, **inputs)
        in_maps.append(m)
        if half == 0:
            idxs = (midx, iidx)
        else:
            assert midx == idxs[0] and iidx == idxs[1], "pool table index maps must match across halves"
    _POOL_IDX[0], _POOL_IDX[1] = idxs
    nc = build_program()
    res = run_bass_kernel_spmd(nc, in_maps, core_ids=list(range(8)))
    _CACHE["res"] = res
    out = np.zeros((4, SEQ, D), np.float32)
    for core in range(8):
        b, half = core // 2, core % 2
        o = np.asarray(res.results[core]["out"], dtype=np.float32)
        if half == 0:
            out[b, 0:TOWN] = o
        else:
            out[b, TOWN:] = o[::-1]
    return out
```

```python
import numpy as np
import ml_dtypes
from contextlib import ExitStack
import concourse.bass as bass
import concourse.mybir as mybir
from concourse.bass_utils import run_bass_kernel_spmd

F32 = mybir.dt.float32
BF16 = mybir.dt.bfloat16
ALU = mybir.AluOpType
AF = mybir.ActivationFunctionType

DEBUG = {}
ENGS = ("pe", "act", "dve", "pool", "sp")
N_DMA_SEMS = 24
BIG = 1.0e30
EPS = 1e-6

D = 1024
SEQ = 4096
TOWN = 2048
NTILE = 16
XW = 2560
CTX = 256
PROJ = 5152
OFF_PZ, OFF_SZ, OFF_XBC, OFF_DT = 1024, 2048, 3072, 5120


def ap_key(ap):
    dims = list(ap.ap)
    off = int(ap.offset)
    if "DRAM" not in str(ap.space).upper():
        ps = int(dims[0][0])
        if ps > 0:
            off = off % ps
    span = 0
    for st, cnt in dims[1:]:
        span += abs(int(st)) * (int(cnt) - 1)
    return ap.tensor.name, off, off + span


class Op:
    __slots__ = ("eng", "fn", "deps", "signal", "sig_idx", "is_dma", "dma_sem", "dma_val", "idx", "raw", "pos")

    def __init__(self, eng, fn, is_dma):
        self.eng, self.fn, self.is_dma = eng, fn, is_dma
        self.deps = set()
        self.raw = set()
        self.pos = -1
        self.signal = False
        self.sig_idx = 0
        self.dma_sem = -1
        self.dma_val = 0
        self.idx = -1


class Sched:
    def __init__(self, nc):
        self.nc = nc
        self.ops = []
        self.hist = {}
        self.n_dma = 0
        self.marks = []

    def mark(self, name):
        self.marks.append((name, len(self.ops)))

    def _touch(self, op, aps, is_write):
        for ap in aps:
            name, lo, hi = ap_key(ap)
            real_w = is_write
            if "PSUM" in str(ap.space).upper():
                lo, hi, is_write = 0, 1 << 30, True
            lst = self.hist.get(name, [])
            keep = []
            for e in lst:
                elo, ehi, eidx, ew, erw = e
                if eidx != op.idx:
                    if not (ehi < lo or hi < elo) and (is_write or ew):
                        op.deps.add(eidx)
                        if erw and not real_w:
                            op.raw.add(eidx)
                    if is_write and lo <= elo and ehi <= hi:
                        continue
                keep.append(e)
            keep.append((lo, hi, op.idx, is_write, real_w))
            self.hist[name] = keep

    def add(self, eng, fn, reads=(), writes=(), dma=False):
        op = Op(eng, fn, dma)
        op.idx = len(self.ops)
        self.ops.append(op)
        self._touch(op, [a for a in reads if a is not None], False)
        self._touch(op, [a for a in writes if a is not None], True)
        if dma:
            op.dma_sem = self.n_dma % N_DMA_SEMS
            op.dma_val = 16 * (self.n_dma // N_DMA_SEMS + 1)
            self.n_dma += 1
        return op

    def emit(self):
        nc, ops = self.nc, self.ops
        last_on_sem = {}
        for op in ops:
            if op.is_dma:
                p = last_on_sem.get(op.dma_sem)
                if p is not None:
                    op.deps.add(p)
                last_on_sem[op.dma_sem] = op.idx
        per_eng0 = {e: [o for o in ops if o.eng == e] for e in ENGS}
        for e in ENGS:
            for i, o in enumerate(per_eng0[e]):
                o.pos = i

        def same_engine_ok(op, dop):
            if op.is_dma or dop.is_dma or dop.eng != op.eng:
                return False
            if op.eng == "pe":
                return True
            if op.eng in ("act", "dve"):
                return (dop.idx not in op.raw) or (op.pos - dop.pos >= 2)
            return False

        self._same_ok = same_engine_ok
        for op in ops:
            for d in op.deps:
                dop = ops[d]
                if not dop.is_dma:
                    if same_engine_ok(op, dop):
                        continue
                    dop.signal = True
        cnt = {e: 0 for e in ENGS}
        for op in ops:
            if op.signal and not op.is_dma:
                cnt[op.eng] += 1
                op.sig_idx = cnt[op.eng]
        per_eng = {e: [o for o in ops if o.eng == e] for e in ENGS}
        dma_tot = {}
        for op in ops:
            if op.is_dma:
                dma_tot[op.dma_sem] = max(dma_tot.get(op.dma_sem, 0), op.dma_val)
        with ExitStack() as st:
            esem = {e: st.enter_context(nc.semaphore("s_" + e)) for e in ENGS}
            dsem = [st.enter_context(nc.semaphore("d%d" % i)) for i in range(N_DMA_SEMS)]
            block = st.enter_context(nc.Block())
            engobj = {"pe": nc.tensor, "act": nc.scalar, "dve": nc.vector, "pool": nc.gpsimd, "sp": nc.sync}

            def run(e):
                eng = engobj[e]
                waited = {}
                for op in per_eng[e]:
                    need = {}
                    for d in op.deps:
                        dop = ops[d]
                        if dop.is_dma:
                            k, v = ("d", dop.dma_sem), dop.dma_val
                        else:
                            if same_engine_ok(op, dop):
                                continue
                            k, v = ("e", dop.eng), dop.sig_idx
                        if need.get(k, 0) < v:
                            need[k] = v
                    for k, v in need.items():
                        if waited.get(k, 0) >= v:
                            continue
                        waited[k] = v
                        eng.wait_ge(dsem[k[1]] if k[0] == "d" else esem[k[1]], v)
                    ins = op.fn(eng)
                    if op.is_dma:
                        ins.then_inc(dsem[op.dma_sem], 16)
                    elif op.signal:
                        ins.then_inc(esem[e], 1)
                if e == "sp":
                    for s, tot in dma_tot.items():
                        eng.wait_ge(dsem[s], tot)

            @block.sync
            def _(sync):
                run("sp")

            @block.scalar
            def _(scalar):
                run("act")

            @block.vector
            def _(vector):
                run("dve")

            @block.gpsimd
            def _(gpsimd):
                run("pool")

            @block.tensor
            def _(tensor):
                run("pe")


def _pool_tables(half):
    wins = (2, 4, 8, 16)
    nspec = (1, 1, 2, 4)
    drange = ((-1, 1), (-1, 1), (-2, 2), (-4, 4))
    mats, midx, invs, iidx = [], {}, [], {}
    for g, w in enumerate(wins):
        lo_off, hi_off = (-(w // 2), w - w // 2 - 1) if half == 0 else (-(w - w // 2 - 1), w // 2)

        def rng_(r):
            return max(r + lo_off, 0), min(r + hi_off, 63)

        def mat(j, d):
            i = j + d
            M = np.zeros((128, 128), np.float32)
            cnt = np.zeros(128, np.float64)
            for t in range(128):
                r, c = 2 * j + t // 64, t % 64
                r0, r1 = rng_(r)
                c0, c1 = rng_(c)
                cnt[t] = (r1 - r0 + 1) * (c1 - c0 + 1)
                for rp in range(max(r0, 2 * i), min(r1, 2 * i + 1) + 1):
                    M[(rp - 2 * i) * 64 + c0:(rp - 2 * i) * 64 + c1 + 1, t] = 1.0
                if d == 0:
                    M[t, t] -= cnt[t]
            return M, cnt

        band = {}
        for d in range(drange[g][0], drange[g][1] + 1):
            if d == 0:
                continue
            band[d] = len(mats)
            mats.append(mat(8, d)[0])
        dg, iv = {}, {}
        for jc in range(nspec[g] + 1):
            M, cnt = mat(jc, 0)
            dg[jc] = len(mats)
            mats.append(M)
            iv[jc] = len(invs)
            invs.append((1.0 / cnt).astype(np.float32))
        for j in range(NTILE):
            jc = min(j, nspec[g])
            iidx[(g, j)] = iv[jc]
            for d in range(drange[g][0], drange[g][1] + 1):
                if j + d < 0:
                    continue
                midx[(g, j, d)] = dg[jc] if d == 0 else band[d]
    mats = np.stack(mats, 1).astype(ml_dtypes.bfloat16)
    invs = np.broadcast_to(np.stack(invs, 0)[None], (128, len(invs), 128)).astype(np.float32).copy()
    return mats, midx, invs, iidx


def _consts():
    ident = np.eye(128, dtype=np.float32)
    s = np.arange(128)[:, None]
    l = np.arange(128)[None, :]
    maskf = (l >= s).astype(np.float32)
    maskb = (l <= s).astype(np.float32)
    sel = np.zeros((96, 32, 128), np.float32)
    for k in range(96):
        sel[k, k % 32, :] = 1.0
    return ident, maskf, maskb, sel.astype(ml_dtypes.bfloat16)


_POOL_IDX = [None, None]
_CACHE = {}


def build_program():
    nc = bass.Bass("TRN2", target_bir_lowering=False)
    din = {}

    def inp(name, shape, dt=F32):
        din[name] = nc.dram_tensor(name, list(shape), dt, kind="ExternalInput").ap()
        return din[name]

    xloc = inp("xloc", [SEQ, D])
    ctxl = inp("ctxl", [CTX, D])
    ccol = inp("ccol", [128, 16])
    w_ada = inp("w_ada", [D, 3 * D])
    b_ada = inp("b_ada", [1, 3 * D])
    nwcol = inp("nwcol", [128, 8])
    w_in = inp("w_in", [D, PROJ])
    w_dt = inp("w_dt", [D, 96])
    conv5 = inp("conv5", [128, 16, 5])
    convb = inp("convb", [128, 16])
    dtp = inp("dtp", [96, 4])
    dskc = inp("dskc", [128, 8])
    snwc = inp("snwc", [128, 8])
    pscc = inp("pscc", [128, 8])
    fnw = inp("fnw", [1, D])
    poolw = inp("poolw", [128, 8, 256])
    w_out = inp("w_out", [2 * D, D])
    c_ident = inp("c_ident", [128, 128])
    c_maskf = inp("c_maskf", [128, 128])
    c_maskb = inp("c_maskb", [128, 128])
    c_sel = inp("c_sel", [96, 32, 128], BF16)
    pm0, pidx0, pi0, iidx0 = _pool_tables(0)
    NM, NI = pm0.shape[1], pi0.shape[1]
    c_pm = inp("c_pm", [128, NM, 128], BF16)
    c_pi = inp("c_pi", [128, NI, 128])
    out_d = nc.dram_tensor("out", [TOWN, D], F32, kind="ExternalOutput").ap()
    ydram = nc.dram_tensor("ydram", [16, 128, TOWN], BF16, kind="Internal").ap()
    wbf = nc.dram_tensor("wbf", [D, PROJ], BF16, kind="Internal").ap()
    woutbf = nc.dram_tensor("woutbf", [2 * D, D], BF16, kind="Internal").ap()
    dbg_out = {k: nc.dram_tensor("dbg_" + k, list(v), F32, kind="ExternalOutput").ap() for k, v in DEBUG.items()}

    S = Sched(nc)

    def mm(out, lhsT, rhs, start=True, stop=True, tp=None):
        kw = {} if tp is None else {"tile_position": tp}
        S.add("pe", lambda e: e.matmul(out, lhsT=lhsT, rhs=rhs, start=start, stop=stop, **kw),
              reads=[lhsT, rhs] + ([] if start else [out]), writes=[out])

    def trp(out, in_, ident):
        S.add("pe", lambda e: e.transpose(out=out, in_=in_, identity=ident), reads=[in_, ident], writes=[out])

    def act(out, in_, func, bias=None, scale=None, accum=None, eng="act"):
        kw = {}
        if bias is not None:
            kw["bias"] = bias
        if scale is not None:
            kw["scale"] = scale
        if accum is not None:
            kw["accum_out"] = accum
        rd = [in_] + [a for a in (bias, scale) if a is not None and not isinstance(a, (int, float))]
        S.add(eng, lambda e: e.activation(out=out, in_=in_, func=func, **kw), reads=rd, writes=[out, accum])

    def tt(out, in0, in1, op, eng="dve"):
        S.add(eng, lambda e: e.tensor_tensor(out=out, in0=in0, in1=in1, op=op), reads=[in0, in1], writes=[out])

    def stt(out, in0, scalar, in1, op0, op1):
        rd = [in0, in1] + ([] if isinstance(scalar, (int, float)) else [scalar])
        S.add("dve", lambda e: e.scalar_tensor_tensor(out=out, in0=in0, scalar=scalar, in1=in1, op0=op0, op1=op1),
              reads=rd, writes=[out])

    def ts(out, in0, s1, s2, op0, op1=None, eng="dve"):
        rd = [in0] + [a for a in (s1, s2) if a is not None and not isinstance(a, (int, float))]
        if op1 is None:
            S.add(eng, lambda e: e.tensor_scalar(out=out, in0=in0, scalar1=s1, scalar2=None, op0=op0), reads=rd, writes=[out])
        else:
            S.add(eng, lambda e: e.tensor_scalar(out=out, in0=in0, scalar1=s1, scalar2=s2, op0=op0, op1=op1), reads=rd, writes=[out])

    def cp(out, in_, eng="dve"):
        S.add(eng, lambda e: e.tensor_copy(out=out, in_=in_), reads=[in_], writes=[out])

    def mset(ap, val, eng="pool"):
        S.add(eng, lambda e: e.memset(ap, val), writes=[ap])

    def dma(out, in_, eng="sp", after=()):
        S.add(eng, lambda e: e.dma_start(out=out, in_=in_), reads=[in_] + list(after), writes=[out], dma=True)

    def scan(out, d0, d1, init):
        rd = [d0, d1] + ([] if isinstance(init, (int, float)) else [init])
        S.add("dve", lambda e: e.tensor_tensor_scan(out=out, data0=d0, data1=d1, initial=init, op0=ALU.mult, op1=ALU.add),
              reads=rd, writes=[out])

    def recip(out, in_):
        S.add("dve", lambda e: e.reciprocal(out=out, in_=in_), reads=[in_], writes=[out])

    with ExitStack() as st:
        def sb(n, sh, dt=F32):
            return st.enter_context(nc.sbuf_tensor(n, list(sh), dt))

        def ps(n, sh, dt=F32):
            return st.enter_context(nc.psum_tensor(n, list(sh), dt))

        ident = sb("ident", [128, 128])
        identb = sb("identb", [128, 128], BF16)
        mask2 = sb("mask2", [128, 256])
        maskf = mask2[:, 0:128]
        maskb = mask2[:, 128:256]
        sel = sb("sel", [96, 32, 128], BF16)
        onesb = sb("onesb", [128, 128], BF16)
        ones32 = sb("ones32", [32, 128])
        epsc = sb("epsc", [128, 1])
        smallc = sb("smallc", [128, 80])
        ccs = sb("ccs", [128, 16])
        gcol = sb("gcol", [128, 32])
        gate_b = sb("gate_b", [128, D])
        conv5s = sb("conv5s", [128, 16, 5])
        dtps = sb("dtps", [96, 4])
        acol = sb("acol", [96, 1])
        xnT = sb("xnT", [128, 8 * XW], BF16)
        xnTv = xnT[:].rearrange("p (k t) -> p k t", k=8)
        woutv = xnT[:, 0:16 * D].rearrange("p (k n) -> p k n", k=16)
        Wg = sb("Wg", [128, 8, 768], BF16)
        Wdt = sb("Wdt", [128, 8, 96], BF16)
        pw = sb("pw", [128, 8, 256], BF16)
        rawseg = sb("rawseg", [128, 4 * 2180], BF16)
        rawv = rawseg[:].rearrange("p (c t) -> p c t", c=4)
        utm = rawseg[:, 0:20 * 256].rearrange("p (i c) -> p i c", i=20)
        xbct = sb("xbct", [128, 4 * 2176], BF16)
        xbcv = xbct[:].rearrange("p (c t) -> p c t", c=4)
        pmt = xbct[:, 0:NM * 128].rearrange("p (m t) -> p m t", m=NM)
        zsT = sb("zsT", [128, 2, TOWN], BF16)
        xstm = sb("xstm", [128, 17, 256], BF16)
        btm = sb("btm", [128, 17, 128], BF16)
        hsb = sb("hsb", [128, 16 * 256], BF16)
        hsbv = hsb[:].rearrange("p (c n) -> p c n", c=16)
        dTt = hsb[:, 0:1024].rearrange("p (c t) -> p c t", c=2)
        bwtm = sb("bwtm", [128, 17, 64])
        acum3 = sb("acum3", [96, TOWN], BF16)
        beta3 = sb("beta3", [96, TOWN], BF16)
        ebf = sb("ebf", [128, 2560], BF16)
        cdb = sb("cdb", [128, 17, 32])
        work = sb("work", [128, 4608])
        bfw = sb("bfw", [128, 4096], BF16)
        diag = bfw[:, 0:2560].rearrange("p (a b) -> p a b", a=20)
        fnw_b = work[:, 0:1024]
        pit = work[:, 1024:1024 + NI * 128].rearrange("p (a b) -> p a b", a=NI)
        hT = sb("hT", [128, 2, 256])
        hpart = sb("hpart", [128, 4, 256])
        hctx = sb("hctx", [128, 2, 4, 256])
        stage = sb("stage", [128, 3, D])
        pb = [ps("pb%d" % i, [128, 512]) for i in range(6)]
        pbb = [ps("pbb%d" % i, [128, 1024], BF16) for i in range(2)]

        dbg_cnt = [0]

        def dump(name, ap, shape=None):
            if name in dbg_out:
                dma(dbg_out[name], ap)

        def dump_cast(name, ap_bf, tmp_f32):
            if name in dbg_out:
                cp(tmp_f32, ap_bf, eng="pool")
                dma(dbg_out[name], tmp_f32)

        dma(ident[:], c_ident[:])
        dma(maskf, c_maskf[:])
        dma(maskb, c_maskb[:])
        dma(sel[:], c_sel[:])
        dma(conv5s[:], conv5[:])
        dma(dtps[:], dtp[:])
        dma(ccs[:], ccol[:])
        dma(smallc[:, 0:8], nwcol[:])
        dma(smallc[:, 8:24], convb[:])
        dma(smallc[:, 24:32], dskc[:])
        dma(smallc[:, 32:40], snwc[:])
        dma(smallc[:, 40:48], pscc[:])
        dma(Wdt[:], w_dt.rearrange("(k p) n -> p k n", p=128), eng="pool")
        def cast_weights(urgent, after=()):
            if urgent:
                cols = ((OFF_XBC, OFF_XBC + 1536),)
            else:
                cols = ((OFF_XBC + 1536, OFF_DT), (OFF_SZ, OFF_XBC), (0, OFF_SZ))
            for (c0, c1) in cols:
                for k in range(8):
                    dma(wbf[k * 128:(k + 1) * 128, c0:c1], w_in[k * 128:(k + 1) * 128, c0:c1], eng="pool", after=after)
            if not urgent:
                dma(pw[:], poolw[:], eng="pool", after=after)
                for k in range(16):
                    dma(woutbf[k * 128:(k + 1) * 128, :], w_out[k * 128:(k + 1) * 128, :], eng="pool", after=after)
        cp(identb[:], ident[:], eng="pool")
        mset(onesb[:], 1.0)
        mset(ones32[:], 1.0)
        mset(epsc[:], EPS)
        nw_c = smallc[:, 0:8]
        cb_c = smallc[:, 8:24]
        dsk_c = smallc[:, 24:32]
        snw_c = smallc[:, 32:40]
        psc_c = smallc[:, 40:48]
        act(acol[:], dtps[:, 1:2], AF.Exp)
        ts(acol[:], acol[:], -1.0, None, ALU.mult)

        def build_xnT(*a, **kw):
            for _ in build_xnT_gen(*a, **kw):
                pass

        def build_xnT_gen(src, ntok, col0, gbase, prologue_only=False, skip_prologue=False):
            nt = ntok // 128
            junk = bfw[:, 3072:4096]

            def stats(ti):
                xs_ = stage[:, ti % 3, :]
                dma(xs_, src[ti * 128:(ti + 1) * 128, :])
                ss = smallc[:, 48 + (ti % 3):49 + (ti % 3)]
                rs = smallc[:, 51 + (ti % 3):52 + (ti % 3)]
                act(junk, xs_, AF.Square, accum=ss)
                act(rs, ss, AF.Ln, bias=epsc[:], scale=1.0 / D)
                act(rs, rs, AF.Exp, scale=-0.5)
                ts(xs_, xs_, rs, None, ALU.mult)

            if not skip_prologue:
                stats(0)
                if nt > 1:
                    stats(1)
            if prologue_only:
                return
            for ti in range(nt):
                xs_ = stage[:, ti % 3, :]
                for k in range(8):
                    bank = pb[2 * (ti % 3) + k // 4]
                    trp(bank[:, (k % 4) * 128:(k % 4 + 1) * 128], xs_[:, k * 128:(k + 1) * 128], ident[:])
                for k in range(8):
                    bank = pb[2 * (ti % 3) + k // 4]
                    psrc = bank[:, (k % 4) * 128:(k % 4 + 1) * 128]
                    dst = xnTv[:, k, col0 + ti * 128:col0 + (ti + 1) * 128]
                    if k < 4:
                        act(dst, psrc, AF.Identity, bias=gcol[:, gbase + 8 + k:gbase + 9 + k], scale=gcol[:, gbase + k:gbase + k + 1])
                    else:
                        ts(dst, psrc, gcol[:, gbase + k:gbase + k + 1], gcol[:, gbase + 8 + k:gbase + 9 + k], ALU.mult, ALU.add)
                if ti + 2 < nt:
                    stats(ti + 2)
                yield ti

        build_xnT(ctxl, CTX, 2304, 16, prologue_only=True)
        S.mark('adaln')
        clb = work[:, 0:1024].rearrange("p (k m) -> p k m", k=8)
        ccb = work[:, 1024:2048].rearrange("p (k m) -> p k m", k=8)
        wstA = hctx[:].rearrange("p a b c -> p (a b) c")
        wstB = work[:, 2560:4608].rearrange("p (k n) -> p k n", k=8)
        bstA = hpart[:, 0, :]
        bstB = hpart[:, 1, :]
        modl = work[:, 2048:2048 + 256]
        modc = work[:, 2304:2304 + 256]
        for k in range(8):
            act(clb[:, k, :], ccs[:, k:k + 1].broadcast_to([128, 128]), AF.Silu)
            act(ccb[:, k, 0:64], ccs[:, k:k + 1].broadcast_to([128, 64]), AF.Silu)
            act(ccb[:, k, 64:128], ccs[:, 8 + k:9 + k].broadcast_to([128, 64]), AF.Silu)
        for nb in range(12):
            c0 = nb * 256
            wst = wstA if nb % 2 == 0 else wstB
            bst = bstA if nb % 2 == 0 else bstB
            if nb == 8:
                cast_weights(True)
            dma(wst, w_ada[:, c0:c0 + 256].rearrange("(k p) n -> p k n", p=128))
            dma(bst, b_ada[0:1, c0:c0 + 256].broadcast_to([128, 256]))
            lhs = ccb if nb < 8 else clb
            pacc = pb[nb % 2]
            for k in range(8):
                mm(pacc[:, 0:256], lhs[:, k, :], wst[:, k, :], start=(k == 0), stop=(k == 7))
            tt(modl, pacc[:, 0:256], bst, ALU.add)
            if nb >= 8:
                cp(gate_b[:, c0 - 2048:c0 - 2048 + 256], modl, eng="pool")
            else:
                pbank = pb[2 + (nb % 2)]
                fk0 = (nb % 4) * 2
                pick = ident[:, 0:128].rearrange("p (a b) -> p a b", b=64)[:, :, 0]
                for hh in range(2):
                    mm(pbank[:, 2 * hh:2 * hh + 2], modl[:, hh * 128:(hh + 1) * 128], pick, start=(hh == 0), stop=True)
                for which in range(2):
                    base = which * 16
                    pair = pbank[:, 0:4].rearrange("p (h w) -> p h w", w=2)[:, :, which]
                    if nb < 4:
                        cp(gcol[:, base + 8 + fk0:base + 10 + fk0], pair)
                    else:
                        stt(gcol[:, base + fk0:base + fk0 + 2], pair, 1.0, nw_c[:, fk0:fk0 + 2], ALU.add, ALU.mult)
        dump("gcol", gcol[:])
        dump("gate_b", gate_b[:])

        def load_w(dst, col_list):
            o = 0
            for c0, n in col_list:
                dma(dst[:, :, o:o + n], wbf[:, c0:c0 + n].rearrange("(k p) n -> p k n", p=128))
                o += n

        pbsel = [0]

        def proj_fm(wt, wc0, m, tok0, n):
            bank = pb[pbsel[0] % 2]
            pbsel[0] += 1
            for k in range(8):
                mm(bank[0:m, 0:n], wt[:, k, wc0:wc0 + m], xnTv[:, k, tok0:tok0 + n], start=(k == 0), stop=(k == 7))
            return bank[0:m, 0:n]

        def blocks(n0, n1):
            r = []
            t = n0
            while t < n1:
                r.append((t, min(512, n1 - t)))
                t += 512
            return r

        def wv(i):
            return work[0:96, i * 512:(i + 1) * 512]

        def dt_path(tok0, ntok, mode, tile0, bw_of=None):
            car = None
            for car in dt_path_gen(tok0, ntok, mode, tile0, bw_of=bw_of):
                pass
            return car

        def interleave(*gens):
            gens = list(gens)
            last = [None] * len(gens)
            alive = [True] * len(gens)
            while any(alive):
                for i, gn in enumerate(gens):
                    if alive[i]:
                        try:
                            last[i] = next(gn)
                        except StopIteration:
                            alive[i] = False
            return last

        def dt_path_gen(tok0, ntok, mode, tile0, bw_of=None):
            carry = None
            ones_m_full = wv(8)
            mset(ones_m_full, 1.0, eng="dve")
            if mode == "own":
                for c in range(4):
                    mset(wv(8)[:, c * 128:c * 128 + 1], 0.0, eng="dve")
            for (t0, n) in blocks(tok0, tok0 + ntok):
                bi = (t0 - tok0) // 512
                dtv, adt, P, cum, tmp, lnd, A, B_ = [wv(i)[:, 0:n] for i in range(8)]
                pp = proj_fm(Wdt, 0, 96, t0, n)
                ts(tmp, pp, dtps[:, 0:1], None, ALU.add)
                stt(A, tmp, -1.0, tmp, ALU.mult, ALU.max)
                act(A, A, AF.Exp, scale=-1.0)
                act(A, A, AF.Ln, bias=1.0)
                stt(dtv, tmp, 0.0, A, ALU.max, ALU.add)
                ts(adt, dtv, acol[:, 0:1], None, ALU.mult)
                act(lnd, dtv, AF.Ln)
                if mode == "own":
                    ones_m = wv(8)[:, 0:n]
                    scan(P, ones_m, adt, 0.0)
                    nch = n // 128
                    P3 = P.rearrange("p (c l) -> p c l", l=128)
                    totb = P3[:, :, 127:128].broadcast_to([96, nch, 128])
                    tt(tmp.rearrange("p (c l) -> p c l", l=128), totb, P3, ALU.subtract)
                    tt(tmp, tmp, adt, ALU.add)
                    ts(cum, P, dtps[:, 2:3], None, ALU.mult)
                    stt(cum, tmp, dtps[:, 3:4], cum, ALU.mult, ALU.add)
                    tt(A, lnd, cum, ALU.subtract)
                    tt(tmp.rearrange("p (c l) -> p c l", l=128), totb, cum.rearrange("p (c l) -> p c l", l=128), ALU.subtract)
                    act(tmp, tmp, AF.Exp)
                    tt(B_, dtv, tmp, ALU.mult)
                    cp(A[32:64, :], B_[32:64, :])
                    hi = ebf[0:96, 0:n]
                    md = ebf[0:96, 512:512 + n]
                    cp(hi, cum)
                    tt(tmp, cum, hi, ALU.subtract)
                    cp(md, tmp)
                    tt(lnd, tmp, md, ALU.subtract)
                    a0 = t0 - tok0
                    cp(acum3[0:32, a0:a0 + n], hi[0:32, :])
                    cp(acum3[32:64, a0:a0 + n], md[32:64, :])
                    cp(acum3[64:96, a0:a0 + n], lnd[64:96, :])
                    cdv = hctx[0:32, 1, 0, 0:nch]
                    act(cdv, P3[0:32, :, 127], AF.Exp)
                    for c in range(nch):
                        dg = hctx[0:32, 1, 1, c * 32:(c + 1) * 32]
                        ts(dg, ident[0:32, 0:32], cdv[:, c:c + 1], None, ALU.mult)
                        mm(pb[5][:, c * 32:(c + 1) * 32], ones32[:, :], dg, start=(c == 0), stop=True)
                    cp(cdb[:, tile0 + bi * 4:tile0 + bi * 4 + nch, :], pb[5][:, 0:nch * 32].rearrange("p (c k) -> p c k", k=32))
                else:
                    ones_m = wv(8)[:, 0:n]
                    scan(P, ones_m, adt, 0.0 if carry is None else carry)
                    carry = smallc[0:96, 60 + (bi % 2):61 + (bi % 2)]
                    cp(carry, P[:, n - 1:n])
                    tt(tmp, P, adt, ALU.subtract)
                    act(tmp, tmp, AF.Exp)
                    tt(A, dtv, tmp, ALU.mult)
                    if mode == "ctx":
                        ts(tmp, P, P[:, n - 1:n], -1.0, ALU.subtract, ALU.mult)
                        act(tmp, tmp, AF.Exp)
                        tt(B_, dtv, tmp, ALU.mult)
                        ts(A, A, dtps[:, 3:4], None, ALU.mult)
                        stt(A, B_, dtps[:, 2:3], A, ALU.mult, ALU.add)
                nt_ = n // 128
                for c in range(nt_):
                    trp(pb[5][:, 128 + c * 64:192 + c * 64], A[0:64, c * 128:(c + 1) * 128], ident[0:64, 0:64])
                if bw_of is None:
                    cp(bwtm[:, tile0 + bi * 4:tile0 + bi * 4 + nt_, :], pb[5][:, 128:128 + nt_ * 64].rearrange("p (c k) -> p c k", k=64))
                else:
                    for c in range(nt_):
                        cp(bw_of(bi * 4 + c), pb[5][:, 128 + c * 64:192 + c * 64])
                yield carry

        def conv_stage(*a, **kw):
            for _ in conv_stage_gen(*a, **kw):
                pass

        def conv_stage_gen(g, wt, ncol_chunks, tok0, ntok, zero_left, zero_right, with_c, after_proj=None,
                           raw_of=None, xbc_of=None, build_diag=True, conv_desc=False, n_out=None):
            nchk = 4 if with_c else 3
            if raw_of is None:
                raw_of = lambda ci, a, b: rawv[:, ci, a:b]
            if xbc_of is None:
                xbc_of = lambda ci, a, b: xbcv[:, ci, a:b]
            chs = [2 * g, 2 * g + 1, 8 + g, 12 + g][:nchk]
            if build_diag:
                for ci, ch in enumerate(chs):
                    for k in range(5):
                        ts(diag[:, ci * 5 + k, :], identb[:], conv5s[:, ch, k:k + 1], None, ALU.mult)
            for ci in range(nchk):
                if zero_left:
                    mset(raw_of(ci, 0, 2), 0.0)
                if zero_right:
                    mset(raw_of(ci, 2 + ntok, 4 + ntok), 0.0)
            for (t0, n) in blocks(tok0, tok0 + ntok):
                for ci in range(nchk):
                    pp = proj_fm(wt, ci * 128, 128, t0, n)
                    o0 = 2 + t0 - tok0
                    if ci % 2 == 0:
                        act(raw_of(ci, o0, o0 + n), pp, AF.Identity)
                    else:
                        cp(raw_of(ci, o0, o0 + n), pp)
                yield None
            if after_proj is not None:
                after_proj()
            cblocks = blocks(0, ntok if n_out is None else n_out)
            for (t0, n) in (cblocks[::-1] if conv_desc else cblocks):
                for ci, ch in enumerate(chs):
                    bank = pb[2 + (ci % 2)]
                    for k in range(5):
                        mm(bank[:, 0:n], diag[:, ci * 5 + k, :], raw_of(ci, t0 + k, t0 + k + n), start=(k == 0), stop=(k == 4))
                    act(xbc_of(ci, t0, t0 + n), bank[:, 0:n], AF.Silu, bias=cb_c[:, ch:ch + 1])
                yield None

        def to_tokmajor(ntiles, t_off, xbc_of=None, xs_of=None, b_of=None, first=0, tiles=None):
            if xbc_of is None:
                xbc_of = lambda ci, a, b: xbcv[:, ci, a:b]
            if xs_of is None:
                xs_of = lambda i: xstm[:, i, :]
            if b_of is None:
                b_of = lambda i: btm[:, i, :]
            for i in (range(first, ntiles) if tiles is None else tiles):
                bank = pbb[i % 2]
                for ci in range(3):
                    trp(bank[:, ci * 128:(ci + 1) * 128], xbc_of(ci, t_off + i * 128, t_off + (i + 1) * 128), identb[:])
                if i % 2 == 0:
                    cp(xs_of(i), bank[:, 0:256])
                    act(b_of(i), bank[:, 256:384], AF.Identity)
                else:
                    act(xs_of(i), bank[:, 0:256], AF.Identity)
                    cp(b_of(i), bank[:, 256:384])

        def xdd_of(i, wcol0, g, dst, bw=None, xs=None):
            bw = bwtm[:, i, :] if bw is None else bw
            xs = xstm[:, i, :] if xs is None else xs
            wq = bw[:, wcol0 + 4 * g:wcol0 + 4 * g + 4].unsqueeze(2).broadcast_to([128, 4, 64])
            tt(dst.rearrange("p (h q) -> p h q", h=4), xs.rearrange("p (h q) -> p h q", h=4), wq, ALU.mult)

        xdd0 = bfw[:, 2048:2304]
        xdd1 = bfw[:, 2304:2560]
        def w_xb(g):
            return [(OFF_XBC + 256 * g, 256), (OFF_XBC + 1024 + 128 * g, 128)]

        def w_ssd(g):
            return w_xb(g) + [(OFF_XBC + 1536 + 128 * g, 128), (OFF_SZ + 256 * g, 256)]

        CX0 = 2304
        c_raw = lambda ci, a, b: rawv[:, 3, ci * 264 + a:ci * 264 + b]
        c_xbc = lambda ci, a, b: xbcv[:, 3, ci * 256 + a:ci * 256 + b]
        c_xs = lambda i: hsb[:, i * 256:(i + 1) * 256]
        c_b = lambda i: hsb[:, 512 + i * 128:512 + (i + 1) * 128]
        c_bw = lambda i: stage[:, 2, i * 64:(i + 1) * 64]
        build_xnT(ctxl, CTX, CX0, 16, skip_prologue=True)
        S.mark('partner')
        def staged(gx, gd, gc, ntiles, dt_after, cv_after):
            car = None
            for ti in range(ntiles):
                next(gx)
                if ti in dt_after:
                    car = next(gd)
                if ti in cv_after:
                    next(gc)
            for _ in gx:
                pass
            for car2 in gd:
                car = car2
            for _ in gc:
                pass
            return car

        gx = build_xnT_gen(xloc[1920:4096, :], 2176, 0, 0)
        next(gx)
        next(gx)
        load_w(Wg, w_xb(0))
        gen0 = conv_stage_gen(0, Wg, 3, 126, 2050, False, True, False)
        car = staged(gx, dt_path_gen(128, TOWN, "part", 1), gen0, 15, (4, 8, 12), (3, 7, 11))
        dt_path(CX0, CTX, "ctx", 0, bw_of=c_bw)
        cast_weights(False, after=[xnTv[:, 7, 2048:2176]])
        cdv = work[0:32, 4096:4097]
        act(cdv, car[0:32, :], AF.Exp)
        dgp = work[0:32, 4160:4192]
        ts(dgp, ident[0:32, 0:32], cdv, None, ALU.mult)
        mm(pb[5][:, 0:32], ones32[:, :], dgp, start=True, stop=True)
        cp(cdb[:, 0, :], pb[5][:, 0:32])
        build_xnT(xloc[0:XW, :], XW, 0, 0, prologue_only=True)
        for g in range(4):
            gc = conv_stage_gen(g, Wg, 3, CX0, CTX, True, True, False, raw_of=c_raw, xbc_of=c_xbc, build_diag=False)
            nxt = w_xb(g + 1) if g < 3 else w_ssd(0)
            if g > 0:
                gp = conv_stage_gen(g, Wg, 3, 126, 2050, False, True, False)
                for _ in range(5):
                    next(gp)
                next(gc)
                load_w(Wg, nxt)
                for _ in gp:
                    pass
                for _ in gc:
                    pass
            else:
                next(gc)
                load_w(Wg, nxt)
                for _ in gc:
                    pass
            to_tokmajor(17, -126, first=1)
            for i in range(1, 17):
                xd = xdd0 if i % 2 == 0 else xdd1
                xdd_of(i, 32 + 16, g, xd)
                mm(pb[4][:, 0:256], btm[:, i, :], xd, start=(i == 1), stop=(i == 16))
            to_tokmajor(2, 0, xbc_of=c_xbc, xs_of=c_xs, b_of=c_b)
            for d in range(2):
                for i in range(2):
                    xd = xdd0 if i % 2 == 0 else xdd1
                    xdd_of(i, 32 + 16 * d, g, xd, bw=c_bw(i), xs=c_xs(i))
                    mm(pb[5][:, 256:512], c_b(i), xd, start=(i == 0), stop=(i == 1))
                cp(hctx[:, d, g, :], pb[5][:, 256:512])
            tt(hpart[:, g, :].rearrange("p (h q) -> p h q", h=4), hctx[:, 1, g, :].rearrange("p (h q) -> p h q", h=4),
               cdb[:, 0, 16 + 4 * g:16 + 4 * g + 4].unsqueeze(2).broadcast_to([128, 4, 64]), ALU.mult)
            tt(hpart[:, g, :], hpart[:, g, :], pb[4][:, 0:256], ALU.add)
        dump("hpart", hpart[:].rearrange("p a b -> p (a b)"))

        S.mark('own_xn_dt')
        gxo = build_xnT_gen(xloc[0:XW, :], XW, 0, 0, skip_prologue=True)
        dump_cast("xnT", xnTv[:, 0, 0:512], work[:, 0:512])
        staged(gxo, dt_path_gen(0, TOWN, "own", 0), conv_stage_gen(0, Wg, 4, 0, TOWN + 2, True, False, True, n_out=TOWN),
               20, (4, 8, 12, 16), (5, 9, 13, 17))
        dump("bwtm", bwtm[:, 0:16, :].rearrange("p a b -> p (a b)"))
        dump("cdb", cdb[:, 0:16, :].rearrange("p a b -> p (a b)"))

        Ef = ebf[:, 0:512]
        Eb = ebf[:, 512:1024]
        EAf = ebf[:, 1024:1536]
        EAb = ebf[:, 1536:2048]
        rstd = work[:, 3584:4096]
        Gf = bfw[:, 0:512]
        Gb = bfw[:, 512:1024]
        Qf = bfw[:, 1024:1536]
        Qb = bfw[:, 1536:2048]
        hTb = bfw[:, 2560:2816]
        sqb = bfw[:, 3072:4096].rearrange("p (a t) -> p a t", a=2)

        cbms = [(ebf[:, 2048:2176], ebf[:, 2176:2304]), (ebf[:, 2304:2432], ebf[:, 2432:2560])]
        dirbuf = ((pb[2], Ef, EAf, Gf, Qf), (pb[3], Eb, EAb, Gb, Qb))

        def w_pool(g):
            return [(256 * g, 256), (OFF_PZ + 256 * g, 256)]

        for g in range(4):
            S.mark('ssd_g%d' % g)
            def pre_s(c, slot):
                xd = bfw[:, 2560:2816] if c % 2 == 0 else bfw[:, 2816:3072]
                xdd_of(c, 32 + 16, g, xd)
                bank = pb[4 + slot // 2]
                mm(bank[:, (slot % 2) * 256:(slot % 2 + 1) * 256], btm[:, c, :], xd, start=(slot % 2 == 0), stop=True)

            def pre_rec(c, slot):
                bank = pb[4 + slot // 2]
                cp(hsbv[:, c, :], hT[:, 1, :])
                tt(hT[:, 1, :].rearrange("p (h q) -> p h q", h=4), hT[:, 1, :].rearrange("p (h q) -> p h q", h=4),
                   cdb[:, c, 16 + 4 * g:16 + 4 * g + 4].unsqueeze(2).broadcast_to([128, 4, 64]), ALU.mult)
                tt(hT[:, 1, :], hT[:, 1, :], bank[:, (slot % 2) * 256:(slot % 2 + 1) * 256], ALU.add)

            zunits = [(t0, n, cc) for (t0, n) in blocks(0, TOWN) for cc in range(2)]

            def zunit(u):
                t0, n, cc = zunits[u]
                pp = proj_fm(Wg, 512 + cc * 128, 128, t0, n)
                act(zsT[:, cc, t0:t0 + n], pp, AF.Silu)

            S.mark('ssd_g%d_pre' % g)
            cp(hT[:, 1, :], hpart[:, g, :])
            if g > 0:
                gp = conv_stage_gen(g, Wg, 4, 0, TOWN + 2, True, False, True, conv_desc=True, n_out=TOWN)
                for _ in range(5):
                    next(gp)
                blist = (3, 2, 1, 0)
            else:
                blist = (3, 2, 1, 0)
                to_tokmajor(16, 0)
            for bblk in blist:
                if g > 0:
                    next(gp)
                    to_tokmajor(16, 0, tiles=[4 * bblk + 3, 4 * bblk + 2, 4 * bblk + 1, 4 * bblk])
                chs = [4 * bblk + 3, 4 * bblk + 2, 4 * bblk + 1, 4 * bblk]
                for slot, c in enumerate(chs):
                    pre_s(c, slot)
                zunit(2 * bblk)
                zunit(2 * bblk + 1)
                for slot, c in enumerate(chs):
                    pre_rec(c, slot)
            if g > 0:
                for _ in gp:
                    pass
            load_w(Wg, w_ssd(g + 1) if g < 3 else [(0, 512), (OFF_PZ, 256)])
            S.mark('ssd_g%d_main' % g)
            cp(hT[:, 0, :], hctx[:, 0, g, :])
            cp(hTb, hctx[:, 0, g, :])

            def tks(c):
                return slice(c * 128, (c + 1) * 128)

            ygs = [work[:, 2560:3584].rearrange("p (a t) -> p a t", a=2), work[:, 0:1024].rearrange("p (a t) -> p a t", a=2)]
            ytmp2 = work[:, 4096:4352].rearrange("p (a t) -> p a t", a=2)

            def cb_front(c):
                tk = tks(c)
                mm(pb[1][:, 0:128], xbcv[:, 2, tk], xbcv[:, 3, tk], start=True, stop=True)

            def cbm_of(c):
                base = 2048 + 256 * (c % 2)
                tt(ebf[:, base:base + 256].rearrange("p (d l) -> p d l", d=2),
                   pb[1][:, 0:128].unsqueeze(1).broadcast_to([128, 2, 128]),
                   mask2[:, :].rearrange("p (d l) -> p d l", d=2), ALU.mult)

            def ps_front(c):
                tk = tks(c)
                for d in range(2):
                    PS = dirbuf[d][0]
                    for h in range(4):
                        dh = 16 * d + 4 * g + h
                        mm(PS[:, h * 128:(h + 1) * 128], sel[:, dh, :], acum3[:, tk], start=(h == 0), stop=True)

            def ea_of(c, d):
                PS, E, EA, G, Q = dirbuf[d]
                act(EA, PS[:, :], AF.Exp)

            def bmm_of(c, d):
                pass

            def e_of(c, d):
                PS, E, EA, G, Q = dirbuf[d]
                for h in range(4):
                    dh = 16 * d + 4 * g + h
                    act(E[:, h * 128:(h + 1) * 128], PS[:, h * 128:(h + 1) * 128], AF.Exp, bias=bwtm[:, c, dh:dh + 1])

            def gq(c, d):
                PS, E, EA, G, Q = dirbuf[d]
                cbm = cbms[c % 2][d]
                stt(G.rearrange("p (h l) -> p h l", h=4), E.rearrange("p (h l) -> p h l", h=4), BIG,
                    cbm.unsqueeze(1).broadcast_to([128, 4, 128]), ALU.min, ALU.mult)
                tt(Q.rearrange("p (h l) -> p h l", h=4), EA.rearrange("p (h l) -> p h l", h=4),
                   xbcv[:, 3, tks(c)].unsqueeze(1).broadcast_to([128, 4, 128]), ALU.mult)

            def norm_sq(blk):
                yg = ygs[blk % 2]
                for hp in range(2):
                    act(sqb[:, hp, :], yg[:, hp, :], AF.Square)

            def norm_mm(blk):
                for hp in range(2):
                    mm(pb[0][:, :], onesb[:], sqb[:, hp, :], start=(hp == 0), stop=(hp == 1))

            def norm_rs(blk):
                act(rstd, pb[0][:, :], AF.Ln, bias=epsc[:], scale=1.0 / 256.0)
                act(rstd, rstd, AF.Exp, scale=-0.5)

            def norm_out(blk):
                yg = ygs[blk % 2]
                for hp in range(2):
                    pr = 2 * g + hp
                    ys = sqb[:, hp, :]
                    stt(ys, yg[:, hp, :], snw_c[:, pr:pr + 1], rstd, ALU.mult, ALU.mult)
                    dma(ydram[8 + pr, :, blk * 512:(blk + 1) * 512], ys, eng="pool")

            def ys_bank(c):
                return pb[4 + (c % 2)]

            def upd(c):
                tt(hT[:, 0, :].rearrange("p (h q) -> p h q", h=4), hT[:, 0, :].rearrange("p (h q) -> p h q", h=4),
                   cdb[:, c, 4 * g:4 * g + 4].unsqueeze(2).broadcast_to([128, 4, 64]), ALU.mult)
                tt(hT[:, 0, :], hT[:, 0, :], ys_bank(c)[:, 256:512], ALU.add)
                cp(hTb, hT[:, 0, :])

            def ymm(c):
                bank = ys_bank(c)
                for hp in range(2):
                    Y = bank[:, hp * 128:(hp + 1) * 128]
                    for hh in range(2):
                        h = 2 * hp + hh
                        o = Y[hh * 64:(hh + 1) * 64, :]
                        tp = (0, 64 * hh)
                        xs_h = xstm[:, c, h * 64:(h + 1) * 64]
                        mm(o, xs_h, Gf[:, h * 128:(h + 1) * 128], start=(hp == 0), stop=False, tp=tp)
                        mm(o, hTb[:, h * 64:(h + 1) * 64], Qf[:, h * 128:(h + 1) * 128], start=False, stop=False, tp=tp)
                        mm(o, xs_h, Gb[:, h * 128:(h + 1) * 128], start=False, stop=False, tp=tp)
                        mm(o, hsbv[:, c, h * 64:(h + 1) * 64], Qb[:, h * 128:(h + 1) * 128], start=False, stop=True, tp=tp)
                xd = xdd0 if c % 2 == 0 else xdd1
                mm(bank[:, 256:512], btm[:, c, :], xd, start=False, stop=True)

            def evac(c):
                tk = tks(c)
                yg = ygs[(c // 4) % 2]
                bank = ys_bank(c)
                for hp in range(2):
                    pr = 2 * g + hp
                    stt(ytmp2[:, hp, :], xbcv[:, hp, tk], dsk_c[:, pr:pr + 1], bank[:, hp * 128:(hp + 1) * 128], ALU.mult, ALU.add)
                tt(yg[:, :, (c % 4) * 128:(c % 4 + 1) * 128], ytmp2, zsT[:, :, tk], ALU.mult)

            cb_front(0)
            cbm_of(0)
            cb_front(1)
            cbm_of(1)
            ps_front(0)
            for d in range(2):
                ea_of(0, d)
                bmm_of(0, d)
                e_of(0, d)
            for c in range(17):
                pend = ((c - 1) // 4 - 1) if ((c - 1) % 4 == 0 and c > 1) else None
                if c < 16:
                    if c + 2 < 16:
                        cb_front(c + 2)
                    if c + 1 < 16:
                        ps_front(c + 1)
                if pend is not None:
                    norm_mm(pend)
                if c < 16:
                    gq(c, 0)
                    if c + 1 < 16:
                        ea_of(c + 1, 0)
                        bmm_of(c + 1, 0)
                        e_of(c + 1, 0)
                    gq(c, 1)
                    if c + 1 < 16:
                        ea_of(c + 1, 1)
                if pend is not None:
                    norm_rs(pend)
                if c < 16:
                    xdd_of(c, 32, g, xdd0 if c % 2 == 0 else xdd1)
                if 1 <= c < 16:
                    upd(c - 1)
                if c < 16:
                    ymm(c)
                    if c + 1 < 16:
                        bmm_of(c + 1, 1)
                        e_of(c + 1, 1)
                    if c + 2 < 16:
                        cbm_of(c + 2)
                if c >= 1:
                    evac(c - 1)
                    if (c - 1) % 4 == 3:
                        norm_sq((c - 1) // 4)
                if pend is not None:
                    norm_out(pend)
            norm_mm(3)
            norm_rs(3)
            norm_out(3)

        S.mark('pool')
        dma(pmt, c_pm[:])
        dma(pit, c_pi[:])
        midx, iidx = _POOL_IDX
        utmA = utm
        utmB = xbct[:, NM * 128:NM * 128 + 20 * 256].rearrange("p (i c) -> p i c", i=20)

        def w_pool_pair(gp):
            return [(256 * gp, 512), (OFF_PZ + 256 * gp, 256)]

        def pool_z(g):
            for (t0, n) in blocks(0, TOWN):
                for cc in range(2):
                    pp = proj_fm(Wg, 512 + cc * 128, 128, t0, n)
                    act(zsT[:, cc, t0:t0 + n], pp, AF.Silu)

        def pool_body(g, utm):
            for blk in range(4):
                for cc in range(2):
                    pbank = pb[2 * (blk % 2) + cc]
                    for jj in range(4):
                        j = blk * 4 + jj
                        ds = sorted(d for (gg, j2, d) in midx if gg == g and j2 == j)
                        o = pbank[:, jj * 128:(jj + 1) * 128]
                        for di, d in enumerate(ds):
                            mm(o, utm[:, j + d, cc * 128:(cc + 1) * 128], pmt[:, midx[(g, j, d)], :],
                               start=(di == 0 and jj == 0), stop=(di == len(ds) - 1))
                    ii = [iidx[(g, blk * 4 + jj)] for jj in range(4)]
                    d3 = dTt[:, cc, :].rearrange("p (j t) -> p j t", j=4)
                    p3 = pbank[:, :].rearrange("p (j t) -> p j t", j=4)
                    if len(set(ii)) == 1:
                        tt(d3, p3, pit[:, ii[0]:ii[0] + 1, :].broadcast_to([128, 4, 128]), ALU.mult)
                    elif ii == list(range(ii[0], ii[0] + 4)):
                        tt(d3, p3, pit[:, ii[0]:ii[0] + 4, :], ALU.mult)
                    else:
                        for jj in range(4):
                            tt(d3[:, jj, :], p3[:, jj, :], pit[:, ii[jj], :], ALU.mult)
                t0 = blk * 512
                for oc in range(2):
                    lb = pb[4 + oc]
                    for cc in range(2):
                        mm(lb[:, :], pw[:, 2 * g + cc, oc * 128:(oc + 1) * 128], dTt[:, cc, :], start=(cc == 0), stop=(cc == 1))
                    yp = bfw[:, (2 * (blk % 2) + oc) * 512:(2 * (blk % 2) + oc + 1) * 512]
                    pr = 2 * g + oc
                    stt(yp, lb[:, :], psc_c[:, pr:pr + 1], zsT[:, oc, t0:t0 + 512], ALU.mult, ALU.mult)
                    dma(ydram[pr, :, t0:t0 + 512], yp, eng="pool")


        for gp in (0, 2):
            for i in range(17 if gp == 0 else 20):
                bank = pb[i % 2]
                for k in range(8):
                    mm(bank[:, 0:512], xnTv[:, k, i * 128:(i + 1) * 128], Wg[:, k, 0:512], start=(k == 0), stop=(k == 7))
                cp(utmA[:, i, :], bank[:, 0:256])
                act(utmB[:, i, :], bank[:, 256:512], AF.Identity)
            pool_z(gp)
            load_w(Wg[:, :, 512:768], [(OFF_PZ + 256 * (gp + 1), 256)])
            pool_body(gp, utmA)
            pool_z(gp + 1)
            if gp == 0:
                load_w(Wg, w_pool_pair(2))
            pool_body(gp + 1, utmB)

        S.mark('outproj')
        dma(woutv, woutbf.rearrange("(k p) n -> p k n", p=128))
        dma(fnw_b, fnw[0:1, :].broadcast_to([128, D]))
        ystA = rawseg[:, 0:16 * 512].rearrange("p (k t) -> p k t", k=16)
        ystB = xbct[:, 0:16 * 512].rearrange("p (k t) -> p k t", k=16)
        def ld_yst(blk):
            dma(ystA if blk % 2 == 0 else ystB, ydram[:, :, blk * 512:(blk + 1) * 512].rearrange("k p t -> p k t"))

        def ld_x(i):
            dma(stage[:, i % 2, :], xloc[i * 128:(i + 1) * 128, :])

        ld_yst(0)
        ld_x(0)
        for blk in range(4):
            yst = ystA if blk % 2 == 0 else ystB
            if blk + 1 < 4:
                ld_yst(blk + 1)
            for jj in range(4):
                i = blk * 4 + jj
                if i + 1 < 16:
                    ld_x(i + 1)
                xs_ = stage[:, i % 2, :]
                hh_ = work[:, 1024 + (i % 2) * 1024:2048 + (i % 2) * 1024]
                oo_ = stage[:, 2, :] if i % 2 == 0 else work[:, 3072:4096]
                pbo = 2 * (i % 2)
                for nb in range(2):
                    for k in range(16):
                        mm(pb[pbo + nb][:, :], yst[:, k, jj * 128:(jj + 1) * 128], woutv[:, k, nb * 512:(nb + 1) * 512],
                           start=(k == 0), stop=(k == 15))
                for nb in range(2):
                    sl = slice(nb * 512, (nb + 1) * 512)
                    tt(hh_[:, sl], pb[pbo + nb][:, :], gate_b[:, sl], ALU.mult)
                tt(hh_, hh_, xs_, ALU.add)
                ss = smallc[:, 56 + 2 * (i % 2):57 + 2 * (i % 2)]
                rs = smallc[:, 57 + 2 * (i % 2):58 + 2 * (i % 2)]
                act(oo_, hh_, AF.Square, accum=ss)
                act(rs, ss, AF.Ln, bias=epsc[:], scale=1.0 / D)
                act(rs, rs, AF.Exp, scale=-0.5)
                stt(oo_, hh_, rs, fnw_b, ALU.mult, ALU.mult)
                dma(out_d[i * 128:(i + 1) * 128, :], oo_, eng="pool")
        S.emit()
    _CACHE['sched'] = S
    return nc


def _prep_core(b, half, x, c, ctx, c_ctx, norm_w, w_ada, b_ada, w_in, conv_w, conv_b, a_log, dt_bias,
               d_skip, ssd_norm_w, pool_w, pool_scale, w_out, final_norm_w, consts):
    f32 = np.float32

    def col8(v):
        return np.ascontiguousarray(v.reshape(-1, 128).T).astype(f32)

    xl = x[b][::-1] if half else x[b]
    cl = ctx[b][::-1] if half else ctx[b]
    W = w_in[0]
    dl = [half, 1 - half]
    wdt = np.concatenate([W[:, OFF_DT + 16 * dl[0]:OFF_DT + 16 * dl[0] + 16],
                          W[:, OFF_DT + 16 * dl[1]:OFF_DT + 16 * dl[1] + 16]], 1)
    wdt = np.concatenate([wdt, wdt, wdt], 1)
    cw = conv_w[0]
    z = np.zeros_like(cw[0])
    taps = [cw[0], cw[1], cw[2], cw[3], z] if half == 0 else [z, cw[3], cw[2], cw[1], cw[0]]
    conv5 = np.stack([t.reshape(16, 128).T for t in taps], 2)
    dtb = np.concatenate([dt_bias[0, dl[0]], dt_bias[0, dl[1]]])
    alg = np.concatenate([a_log[0, dl[0]], a_log[0, dl[1]]])
    mf = np.concatenate([np.ones(16), np.zeros(16)])
    dtp = np.stack([np.tile(dtb, 3), np.tile(alg, 3), np.tile(mf, 3), np.tile(1 - mf, 3)], 1)
    dsk = np.repeat(d_skip[0].reshape(8, 2, 1), 64, 2).reshape(8, 128).T
    ident, maskf, maskb, sel = consts
    pm, midx, pi, iidx = _pool_tables(half)
    pwl = pool_w[0].reshape(4, 2, 128, 256).transpose(2, 0, 1, 3).reshape(128, 8, 256)
    m = {
        "xloc": xl, "ctxl": cl,
        "ccol": np.concatenate([col8(c[b]), col8(c_ctx)], 1),
        "w_ada": w_ada[0], "b_ada": b_ada[0][None, :],
        "nwcol": col8(norm_w[0]), "w_in": W, "w_dt": wdt,
        "conv5": conv5, "convb": col8(conv_b[0]), "dtp": dtp, "dskc": dsk,
        "snwc": col8(ssd_norm_w[0]), "pscc": col8(pool_scale[0]), "fnw": final_norm_w[None, :],
        "poolw": pwl, "w_out": w_out[0],
        "c_ident": ident, "c_maskf": maskf, "c_maskb": maskb, "c_sel": sel, "c_pm": pm, "c_pi": pi,
    }
    out = {}
    for k, v in m.items():
        if v.dtype == ml_dtypes.bfloat16:
            out[k] = np.ascontiguousarray(v)
        else:
            out[k] = np.ascontiguousarray(v, dtype=f32)
    return out, midx, iidx


def kernel(**inputs):
    inputs = {k: np.asarray(v) for k, v in inputs.items()}
    consts = _consts()
    in_maps = []
    idxs = None
    for core in range(8):
        b, half = core // 2, core % 2
        m, midx, iidx = _prep_core(b, half, consts=consts, **inputs)
        in_maps.append(m)
        if half == 0:
            idxs = (midx, iidx)
        else:
            assert midx == idxs[0] and iidx == idxs[1], "pool table index maps must match across halves"
    _POOL_IDX[0], _POOL_IDX[1] = idxs
    nc = build_program()
    res = run_bass_kernel_spmd(nc, in_maps, core_ids=list(range(8)))
    _CACHE["res"] = res
    out = np.zeros((4, SEQ, D), np.float32)
    for core in range(8):
        b, half = core // 2, core % 2
        o = np.asarray(res.results[core]["out"], dtype=np.float32)
        if half == 0:
            out[b, 0:TOWN] = o
        else:
            out[b, TOWN:] = o[::-1]
    return out
```
